# Optimizing a Trainium2 kernel written in Bass

```python
import jax, jax.numpy as jnp
from jax import lax
import numpy as np

D_MODEL = 4096
BATCH = 16
SEQ = 256
DEPTH = 2
DEC_BATCH = 4
DEC_SEQ = 1024
PAST_LEN = 512

GRID_W = 64
N_EVEN = (DEPTH + 1) // 2
N_ODD = DEPTH // 2
N_SUB = 3
D_FF = 11008
MIX_W = D_MODEL
POOL_GROUPS = 4
POOL_WINDOWS = (2, 4, 8, 16)
POOL_W = MIX_W // 2
POOL_GW = POOL_W // POOL_GROUPS
FNET_GROUPS = 4
FNET_W = MIX_W - POOL_W
FNET_GW = FNET_W // FNET_GROUPS
RWKV_W = MIX_W // 2
RWKV_HEAD = 64
RWKV_H = RWKV_W // RWKV_HEAD
DECAY_LORA = 128
ICLR_LORA = 128
GATE_LORA = 480
RWKV_COLS = 3 * RWKV_W + GATE_LORA + 2 * DECAY_LORA + ICLR_LORA
ATT_W = MIX_W - RWKV_W
HEAD_DIM = 128
N_Q_HEADS = ATT_W // HEAD_DIM
N_KV_HEADS = 4
GQA_GROUP = N_Q_HEADS // N_KV_HEADS
KV_W = N_KV_HEADS * HEAD_DIM
ODD_IN_COLS = RWKV_COLS + ATT_W + 2 * KV_W
Q_BLOCK = 128
ROPE_THETA = 10000.0
ALPHA = (2 * DEPTH) ** 0.25
BETA = (8 * DEPTH) ** -0.25
LN_EPS = 1e-5
GN_EPS = 64e-5

kernel_name = 'hybrid_diffusion_prefix_step'


def layer_norm(x, g, b):
    xf = x.astype(jnp.float32)
    mu = jnp.mean(xf, -1, keepdims=True)
    var = jnp.mean(jnp.square(xf - mu), -1, keepdims=True)
    return ((xf - mu) * lax.rsqrt(var + LN_EPS)).astype(x.dtype) * g + b


def rms_norm(x, g):
    xf = x.astype(jnp.float32)
    return (xf * lax.rsqrt(jnp.mean(jnp.square(xf), -1, keepdims=True) + 1e-6)).astype(x.dtype) * g


def modulate(x, shift, scale):
    return x * (1 + scale) + shift


def deepnorm_residual(x, y, gate, g, b):
    return layer_norm(ALPHA * x + gate * y, g, b)


def swiglu(u, w_in, w_out):
    gate, up = jnp.split(u @ w_in, 2, axis=-1)
    return (jax.nn.silu(gate) * up) @ w_out


def centred_window_mean(u, w):
    n = u.shape[1]
    cs = jnp.cumsum(u.astype(jnp.float32), axis=1)
    cs = jnp.concatenate([jnp.zeros_like(cs[:, :1]), cs], axis=1)
    t = jnp.arange(n)
    lo = jnp.clip(t - w // 2, 0, n)
    hi = jnp.clip(t + w // 2, 0, n)
    cnt = (hi - lo).astype(jnp.float32)[None, :, None]
    return ((cs[:, hi] - cs[:, lo]) / cnt).astype(u.dtype)


def pool_mixer(u, pool_w, pool_scale):
    b, n, _ = u.shape
    ug = u.reshape(b, n, POOL_GROUPS, POOL_GW)
    pooled = jnp.stack([centred_window_mean(ug[:, :, gi], w) for gi, w in enumerate(POOL_WINDOWS)], axis=2) - ug
    y = jnp.einsum('bngc,gcd->bngd', pooled, pool_w)
    return y.reshape(b, n, POOL_W) * pool_scale


def fourier_mixer(u):
    b, n, _ = u.shape
    ug = u.reshape(b, n, FNET_GROUPS, FNET_GW).astype(jnp.float32)
    f = jnp.fft.fft2(ug, axes=(1, 3)).real * (n * FNET_GW) ** -0.5
    return f.reshape(b, n, FNET_W).astype(u.dtype)


def even_mixer(u, w_in, pool_w, pool_scale, w_out):
    p = u @ w_in
    y = jnp.concatenate([pool_mixer(p[..., :POOL_W], pool_w, pool_scale), fourier_mixer(p[..., POOL_W:])], axis=-1)
    return y @ w_out


def token_shift(p, mu):
    zero = jnp.zeros_like(p[:, :1])
    prev = jnp.concatenate([zero, p[:, :-1]], axis=1)
    nxt = jnp.concatenate([p[:, 1:], zero], axis=1)
    return p + (0.5 * (prev + nxt) - p) * mu


def rwkv_step(S, inp):
    w_t, r_t, k_t, v_t, kk_t, kka_t = inp
    sa = jnp.einsum('bhij,bhj->bhi', S, -kk_t)
    S = S * w_t[:, :, None, :] + sa[..., None] * kka_t[:, :, None, :] + v_t[..., None] * k_t[:, :, None, :]
    return S, jnp.einsum('bhij,bhj->bhi', S, r_t)


def rwkv7_mixer(p, shift_mu, decay_w0, decay_w2, iclr_a0, iclr_w2, gate_w2, k_k, k_a, r_k, gn_g, gn_b, init_state):
    b, n, _ = p.shape
    p = token_shift(p, shift_mu)
    splits = [RWKV_W, 2 * RWKV_W, 3 * RWKV_W, 3 * RWKV_W + GATE_LORA, 3 * RWKV_W + GATE_LORA + 2 * DECAY_LORA]
    r, k, v, g_lo, w_lo, a_lo = jnp.split(p, splits, axis=-1)
    w_log = decay_w0 + jnp.einsum('bndr,drc->bndc', jnp.tanh(w_lo.reshape(b, n, 2, DECAY_LORA)), decay_w2)
    w_log = -jax.nn.softplus(-w_log.astype(jnp.float32)) - 0.5
    decay = jnp.exp(-jnp.exp(w_log))
    a = jax.nn.sigmoid(iclr_a0 + a_lo @ iclr_w2)
    g = jax.nn.sigmoid(g_lo) @ gate_w2
    heads = lambda t: t.reshape(b, n, RWKV_H, RWKV_HEAD).astype(jnp.float32)
    kk = heads(k * k_k)
    kk = kk * lax.rsqrt(jnp.sum(jnp.square(kk), -1, keepdims=True) + 1e-12)
    rh, kh, vh, ah = heads(r), heads(k * (1 + (a - 1) * k_a)), heads(v), heads(a)
    tmaj = lambda t: jnp.moveaxis(t, 1, 0)
    shared = (tmaj(rh), tmaj(kh), tmaj(vh), tmaj(kk), tmaj(kk * ah))
    dh = decay.reshape(b, n, 2, RWKV_H, RWKV_HEAD)
    if init_state is None:
        init_state = jnp.zeros((b, 2, RWKV_H, RWKV_HEAD, RWKV_HEAD), jnp.float32)
    init_state = init_state.astype(jnp.float32)
    s_f, y_f = lax.scan(rwkv_step, init_state[:, 0], (tmaj(dh[:, :, 0]),) + shared)
    s_b, y_b = lax.scan(rwkv_step, init_state[:, 1], (tmaj(dh[:, :, 1]),) + shared, reverse=True)
    y = jnp.moveaxis(y_f + y_b, 0, 1)
    mu = jnp.mean(y, -1, keepdims=True)
    var = jnp.mean(jnp.square(y - mu), -1, keepdims=True)
    yn = ((y - mu) * lax.rsqrt(var + GN_EPS)).reshape(b, n, RWKV_W) * gn_g + gn_b
    bonus = (jnp.sum(rh * kh * r_k, -1, keepdims=True) * vh).reshape(b, n, RWKV_W)
    out = (yn + bonus).astype(p.dtype) * g
    return out, jnp.stack([s_f, s_b], axis=1)


def axial_rope(x):
    n = x.shape[1]
    rows = n // GRID_W
    row = jnp.repeat(jnp.arange(rows, dtype=jnp.float32), GRID_W)
    col = jnp.tile(jnp.arange(GRID_W, dtype=jnp.float32), rows)
    half = HEAD_DIM // 2
    quarter = half // 2
    inv_freq = ROPE_THETA ** (-jnp.arange(quarter, dtype=jnp.float32) / quarter)

    def rotate(xh, pos):
        ang = pos[:, None] * inv_freq[None, :]
        cos = jnp.cos(ang)[None, :, None, :]
        sin = jnp.sin(ang)[None, :, None, :]
        x1, x2 = xh[..., :quarter], xh[..., quarter:]
        return jnp.concatenate([x1 * cos - x2 * sin, x2 * cos + x1 * sin], axis=-1)

    xf = x.astype(jnp.float32)
    return jnp.concatenate([rotate(xf[..., :half], row), rotate(xf[..., half:], col)], axis=-1).astype(x.dtype)


def block_attention(q, k, v):
    b, n = q.shape[:2]
    nb = n // Q_BLOCK
    qb = jnp.moveaxis(q.reshape(b, nb, Q_BLOCK, N_KV_HEADS, GQA_GROUP, HEAD_DIM), 1, 0)
    scale = HEAD_DIM ** -0.5

    def one_block(qblk):
        s = jnp.einsum('bqhgd,bkhd->bhgqk', qblk, k).astype(jnp.float32) * scale
        pr = jax.nn.softmax(s, axis=-1).astype(v.dtype)
        return jnp.einsum('bhgqk,bkhd->bqhgd', pr, v)

    return jnp.moveaxis(lax.map(one_block, qb), 0, 1).reshape(b, n, ATT_W)


def odd_mixer(u, w_in, shift_mu, decay_w0, decay_w2, iclr_a0, iclr_w2, gate_w2, k_k, k_a, r_k, gn_g, gn_b,
              q_norm, k_norm, w_out, ctx_k, ctx_v, ctx_state):
    b, n, _ = u.shape
    p = u @ w_in
    p_rwkv, p_q, p_k, p_v = jnp.split(p, [RWKV_COLS, RWKV_COLS + ATT_W, RWKV_COLS + ATT_W + KV_W], axis=-1)
    y_rwkv, s_final = rwkv7_mixer(p_rwkv, shift_mu, decay_w0, decay_w2, iclr_a0, iclr_w2, gate_w2,
                                  k_k, k_a, r_k, gn_g, gn_b, ctx_state)
    q = rms_norm(p_q.reshape(b, n, N_Q_HEADS, HEAD_DIM), q_norm)
    k = rms_norm(p_k.reshape(b, n, N_KV_HEADS, HEAD_DIM), k_norm)
    v = p_v.reshape(b, n, N_KV_HEADS, HEAD_DIM)
    if ctx_k is None:
        o = block_attention(q.reshape(b, n, N_KV_HEADS, GQA_GROUP, HEAD_DIM), k, v)
        ctx_out = (k, v, s_final)
    else:
        q = axial_rope(q)
        keys = jnp.concatenate([ctx_k.astype(k.dtype), axial_rope(k)], axis=1)
        vals = jnp.concatenate([ctx_v.astype(v.dtype), v], axis=1)
        o = block_attention(q.reshape(b, n, N_KV_HEADS, GQA_GROUP, HEAD_DIM), keys, vals)
        ctx_out = None
    return jnp.concatenate([y_rwkv, o], axis=-1) @ w_out, ctx_out


def run_trunk(x, cond, ctx_cache, weights):
    (w_ada, b_ada, ln_g, ln_b, ffn_w_in, ffn_w_out, even_w_in, pool_w, pool_scale, even_w_out,
     odd_w_in, shift_mu, decay_w0, decay_w2, iclr_a0, iclr_w2, gate_w2, k_k, k_a, r_k, gn_g, gn_b,
     q_norm, k_norm, odd_w_out) = weights
    silu_c = jax.nn.silu(cond)
    new_k, new_v, new_s = [], [], []
    for l in range(DEPTH):
        i = l // 2
        mod = (silu_c @ w_ada[l] + b_ada[l]).reshape(-1, 1, N_SUB, 3, D_MODEL)
        shift, scale, gate = mod[:, :, :, 0], mod[:, :, :, 1], mod[:, :, :, 2]
        u = modulate(x, shift[:, :, 0], scale[:, :, 0])
        x = deepnorm_residual(x, 0.5 * swiglu(u, ffn_w_in[l, 0], ffn_w_out[l, 0]), gate[:, :, 0], ln_g[l, 0], ln_b[l, 0])
        u = modulate(x, shift[:, :, 1], scale[:, :, 1])
        if l % 2 == 0:
            y = even_mixer(u, even_w_in[i], pool_w[i], pool_scale[i], even_w_out[i])
        else:
            if ctx_cache is None:
                ctx = (None, None, None)
            else:
                ctx = (ctx_cache[0][:, i], ctx_cache[1][:, i], ctx_cache[2][:, i])
            y, ctx_out = odd_mixer(u, odd_w_in[i], shift_mu[i], decay_w0[i], decay_w2[i], iclr_a0[i], iclr_w2[i],
                                   gate_w2[i], k_k[i], k_a[i], r_k[i], gn_g[i], gn_b[i], q_norm[i], k_norm[i],
                                   odd_w_out[i], ctx[0], ctx[1], ctx[2])
            if ctx_out is not None:
                new_k.append(ctx_out[0])
                new_v.append(ctx_out[1])
                new_s.append(ctx_out[2])
        x = deepnorm_residual(x, y, gate[:, :, 1], ln_g[l, 1], ln_b[l, 1])
        u = modulate(x, shift[:, :, 2], scale[:, :, 2])
        x = deepnorm_residual(x, 0.5 * swiglu(u, ffn_w_in[l, 1], ffn_w_out[l, 1]), gate[:, :, 2], ln_g[l, 2], ln_b[l, 2])
    return x, new_k, new_v, new_s


def setup_inputs(seed: int = 0) -> dict:
    key = jax.random.key(seed)
    ks = iter(jax.random.split(key, 40))

    def nrm(shape, s):
        return jax.random.normal(next(ks), shape, jnp.float32) * s

    def uni(shape, lo, hi):
        return jax.random.uniform(next(ks), shape, jnp.float32, lo, hi)

    D = D_MODEL
    return {
        'x_prompt': nrm((BATCH, SEQ, D), 1.0),
        'x_sample': nrm((DEC_BATCH, DEC_SEQ, D), 1.0),
        'cache_k': nrm((DEC_BATCH, N_ODD, PAST_LEN, N_KV_HEADS, HEAD_DIM), 1.0),
        'cache_v': nrm((DEC_BATCH, N_ODD, PAST_LEN, N_KV_HEADS, HEAD_DIM), 1.0),
        'state_rwkv': nrm((DEC_BATCH, N_ODD, 2, RWKV_H, RWKV_HEAD, RWKV_HEAD), 0.3),
        'c': nrm((DEC_BATCH, D), 1.0),
        'c_ctx': nrm((D,), 1.0),
        'w_ada': nrm((DEPTH, D, N_SUB * 3 * D), 0.5 * D ** -0.5),
        'b_ada': nrm((DEPTH, N_SUB * 3 * D), 0.02),
        'ln_g': 1.0 + nrm((DEPTH, N_SUB, D), 0.02),
        'ln_b': nrm((DEPTH, N_SUB, D), 0.02),
        'ffn_w_in': nrm((DEPTH, 2, D, 2 * D_FF), D ** -0.5),
        'ffn_w_out': nrm((DEPTH, 2, D_FF, D), BETA * D_FF ** -0.5),
        'even_w_in': nrm((N_EVEN, D, MIX_W), D ** -0.5),
        'pool_w': nrm((N_EVEN, POOL_GROUPS, POOL_GW, POOL_GW), POOL_GW ** -0.5),
        'pool_scale': 1.0 + nrm((N_EVEN, POOL_W), 0.1),
        'even_w_out': nrm((N_EVEN, MIX_W, D), BETA * MIX_W ** -0.5),
        'odd_w_in': nrm((N_ODD, D, ODD_IN_COLS), D ** -0.5),
        'shift_mu': uni((N_ODD, RWKV_COLS), 0.0, 1.0),
        'decay_w0': uni((N_ODD, 2, RWKV_W), -3.0, 0.0),
        'decay_w2': nrm((N_ODD, 2, DECAY_LORA, RWKV_W), 0.1 * DECAY_LORA ** -0.5),
        'iclr_a0': nrm((N_ODD, RWKV_W), 0.1),
        'iclr_w2': nrm((N_ODD, ICLR_LORA, RWKV_W), 0.1 * ICLR_LORA ** -0.5),
        'gate_w2': nrm((N_ODD, GATE_LORA, RWKV_W), GATE_LORA ** -0.5),
        'k_k': 0.85 + nrm((N_ODD, RWKV_W), 0.05),
        'k_a': 1.0 + nrm((N_ODD, RWKV_W), 0.05),
        'r_k': nrm((N_ODD, RWKV_H, RWKV_HEAD), 0.1),
        'gn_g': 1.0 + nrm((N_ODD, RWKV_W), 0.02),
        'gn_b': nrm((N_ODD, RWKV_W), 0.02),
        'q_norm': 1.0 + nrm((N_ODD, HEAD_DIM), 0.02),
        'k_norm': 1.0 + nrm((N_ODD, HEAD_DIM), 0.02),
        'odd_w_out': nrm((N_ODD, MIX_W, D), BETA * MIX_W ** -0.5),
    }


def reference(x_prompt, x_sample, cache_k, cache_v, state_rwkv, c, c_ctx, w_ada, b_ada, ln_g, ln_b,
              ffn_w_in, ffn_w_out, even_w_in, pool_w, pool_scale, even_w_out, odd_w_in, shift_mu,
              decay_w0, decay_w2, iclr_a0, iclr_w2, gate_w2, k_k, k_a, r_k, gn_g, gn_b, q_norm, k_norm,
              odd_w_out):
    weights = (w_ada, b_ada, ln_g, ln_b, ffn_w_in, ffn_w_out, even_w_in, pool_w, pool_scale, even_w_out,
               odd_w_in, shift_mu, decay_w0, decay_w2, iclr_a0, iclr_w2, gate_w2, k_k, k_a, r_k, gn_g, gn_b,
               q_norm, k_norm, odd_w_out)
    y_prompt, new_k, new_v, new_s = run_trunk(x_prompt, c_ctx[None, :], None, weights)
    y_sample, _, _, _ = run_trunk(x_sample, c, (cache_k, cache_v, state_rwkv), weights)
    new_cache_k = jnp.stack(new_k, axis=1)
    new_cache_v = jnp.stack(new_v, axis=1)
    new_state_rwkv = jnp.stack(new_s, axis=1)
    return (y_prompt, y_sample, new_cache_k, new_cache_v, new_state_rwkv)
```

```python
import math
import numpy as np
import ml_dtypes
import concourse.bass as bass
import concourse.mybir as mybir
from concourse.bass_utils import run_bass_kernel_spmd

F32 = mybir.dt.float32
BF16 = mybir.dt.bfloat16
AF = mybir.ActivationFunctionType
ALU = mybir.AluOpType
NPBF = ml_dtypes.bfloat16


class Cfg:
    def __init__(self, **kw):
        self.D = 4096
        self.L = 2
        self.NT = 1024
        self.SEQ = 256
        self.DEC_SEQ = 1024
        self.PAST = 512
        self.GRID_W = 64
        self.DFF = 11008
        self.GATE_LORA = 480
        self.DECAY_LORA = 128
        self.ICLR_LORA = 128
        self.NKV = 4
        self.HD = 128
        self.NSAMP = 4
        self.NPROMPT_CORES = 4
        self.POOL_WINDOWS = (2, 4, 8, 16)
        self.stop_after = None
        for k, v in kw.items():
            setattr(self, k, v)
        D = self.D
        self.TH = min(512, self.NT // 2)
        self.NH = self.NT // self.TH
        self.TT = self.NT // 128
        self.DC = D // 128
        self.FC = self.DFF // 128
        self.MIXW = D
        self.POOL_W = D // 2
        self.GW = self.POOL_W // 4
        self.GC = self.GW // 128
        self.RW = D // 2
        self.RH = self.RW // 64
        self.HC = self.RW // 128
        self.ATT_W = D - self.RW
        self.NQ = self.ATT_W // self.HD
        self.GQA = self.NQ // self.NKV
        self.KVW = self.NKV * self.HD
        self.RCOLS = 3 * self.RW + self.GATE_LORA + 2 * self.DECAY_LORA + self.ICLR_LORA
        self.ODD_COLS = self.RCOLS + self.ATT_W + 2 * self.KVW
        self.GLC = (self.GATE_LORA + 127) // 128
        self.NKEY = self.PAST + self.NT
        self.KT = self.NKEY // 128
        self.NCH = self.NT // 128
        self.ALPHA = (2 * self.L) ** 0.25
        self.NSUB = 3


FULL = Cfg()


class Buf:
    def __init__(self, name, t=None):
        self.name = name
        self.t = t
        self.w = {}
        self.r = {}
        self.dma_sem = None
        self.dma_cnt = 0
        self.excl = False

    def __getitem__(self, idx):
        return self.t[idx]


class EngState:
    def __init__(self, name, h, sem):
        self.name = name
        self.h = h
        self.sem = sem
        self.count = 0
        self.waited = {}


class Sched:
    def __init__(self, nc):
        self.nc = nc
        self.sems = {}
        self.E = {}
        for name, h in (("pe", nc.tensor), ("act", nc.scalar), ("dve", nc.vector),
                        ("pool", nc.gpsimd), ("sp", nc.sync)):
            sem = nc.alloc_semaphore(name="sem_" + name)
            self.sems[id(sem)] = sem
            self.E[name] = EngState(name, h, sem)
        self.n_instr = 0
        self.dma_bufs = []
        self.uid = 0

    def sbuf(self, name, shape, dtype):
        self.uid += 1
        return Buf(name, self.nc.alloc_sbuf_tensor(f"{name}_{self.uid}", list(shape), dtype))

    def sbuf_at(self, name, shape, dtype, offset):
        self.uid += 1
        assert offset % 32 == 0, (name, offset)
        return Buf(name, self.nc.alloc_sbuf_tensor_at(f"{name}_{self.uid}", list(shape), dtype, offset=offset))

    def psum(self, name, shape, dtype=F32):
        self.uid += 1
        b = Buf(name, self.nc.alloc_psum_tensor(f"{name}_{self.uid}", list(shape), dtype))
        b.excl = True
        return b

    def region(self, name):
        return Buf(name)

    def _dma_sem(self, b):
        if b.dma_sem is None:
            self.uid += 1
            b.dma_sem = self.nc.alloc_semaphore(name=f"ds_{b.name}_{self.uid}")
            self.sems[id(b.dma_sem)] = b.dma_sem
            self.dma_bufs.append(b)
        return b.dma_sem

    def _deps(self, reads, writes, merge):
        deps = {}

        def upd(d):
            for k, v in d.items():
                if deps.get(k, 0) < v:
                    deps[k] = v
        for b in reads:
            upd(b.w)
            if b.excl:
                upd(b.r)
        for b in writes:
            if not merge:
                upd(b.w)
            upd(b.r)
        return deps

    def _wait(self, eng, deps, skip_own=False):
        for k, v in deps.items():
            if skip_own and k == id(eng.sem):
                continue
            if eng.waited.get(k, 0) < v:
                eng.h.wait_ge(self.sems[k], v)
                eng.waited[k] = v

    def _record(self, ev, reads, writes, merge):
        k, v = ev
        for b in reads:
            if b.r.get(k, 0) < v:
                b.r[k] = v
        for b in writes:
            if merge:
                if b.w.get(k, 0) < v:
                    b.w[k] = v
            else:
                b.w = {k: v}
                b.r = {}

    def op(self, eng_name, fn, reads=(), writes=(), merge=False):
        eng = self.E[eng_name]
        self._wait(eng, self._deps(reads, writes, merge), skip_own=(eng_name == "pe"))
        ins = fn(eng.h)
        eng.count += 1
        ins.then_inc(eng.sem, 1)
        self._record((id(eng.sem), eng.count), reads, writes, merge)
        self.n_instr += 1
        return ins

    def mm(self, out_buf, out_ap, pairs, reads, merge=False, start=True, stop=True, transpose=False):
        eng = self.E["pe"]
        self._wait(eng, self._deps(reads, [out_buf], merge), skip_own=True)
        n = len(pairs)
        ins = None
        for i, (l, r) in enumerate(pairs):
            if transpose:
                ins = eng.h.transpose(out_ap, l, r)
            else:
                ins = eng.h.matmul(out_ap, l, r, start=(start and i == 0), stop=(stop and i == n - 1))
            self.n_instr += 1
        eng.count += 1
        ins.then_inc(eng.sem, 1)
        self._record((id(eng.sem), eng.count), reads, [out_buf], merge)
        return ins

    def dma(self, q, out_ap, in_ap, sem_buf, reads=(), writes=(), merge=False):
        eng = self.E[q]
        self._wait(eng, self._deps(reads, writes, merge))
        sem = self._dma_sem(sem_buf)
        ins = eng.h.dma_start(out=out_ap, in_=in_ap)
        ins.then_inc(sem, 16)
        sem_buf.dma_cnt += 16
        self._record((id(sem), sem_buf.dma_cnt), reads, writes, merge)
        self.n_instr += 1
        return ins

    def barrier(self):
        deps = {}
        for e in self.E.values():
            if e.count:
                deps[id(e.sem)] = e.count
        for b in self.dma_bufs:
            if b.dma_cnt:
                deps[id(b.dma_sem)] = b.dma_cnt
        for e in self.E.values():
            self._wait(e, deps, skip_own=True)

    def finish(self):
        deps = {}
        for e in self.E.values():
            if e.count:
                deps[id(e.sem)] = e.count
        for b in self.dma_bufs:
            if b.dma_cnt:
                deps[id(b.dma_sem)] = b.dma_cnt
        self._wait(self.E["sp"], deps, skip_own=True)


class Rot:
    def __init__(self, bufs):
        self.bufs = bufs
        self.i = 0

    def next(self):
        b = self.bufs[self.i % len(self.bufs)]
        self.i += 1
        return b


class Program:
    def __init__(self, cfg):
        self.cfg = cfg
        c = cfg
        self.nc = nc = bass.Bass("TRN2", target_bir_lowering=False)
        self.S = S = Sched(nc)
        self.din = {}
        self.dout = {}
        self.regions = {}
        D, NT, DC = c.D, c.NT, c.DC

        def inp(name, shape):
            self.din[name] = nc.dram_tensor(name, list(shape), F32, kind="ExternalInput").ap()
            return self.din[name]

        def outp(name, shape):
            self.dout[name] = nc.dram_tensor(name, list(shape), F32, kind="ExternalOutput").ap()
            return self.dout[name]

        def scratch(name, shape, dtype=F32):
            return nc.dram_tensor(name, list(shape), dtype, kind="Internal").ap()

        self.inp, self.outp, self.scratch = inp, outp, scratch
        inp("xT", [D, NT])
        inp("cond", [128, DC])
        inp("w_ada", [c.L * D, 9 * D])
        inp("b_adaT", [128, c.L * 9 * DC])
        inp("lnT", [128, c.L * 3 * 2 * DC])
        inp("ffn_in", [c.L * 2 * c.FC * 128, DC * 256])
        inp("ffn_out", [c.L * 2 * DC * 128, c.FC * 128])
        outp("yT", [D, NT])
        self.X = [self.din["xT"]] + [scratch(f"X{i}", [D, NT]) for i in range(1, c.L * 3)] + [self.dout["yT"]]
        self.Z = scratch("Z", [D, NT])

        self.modT = S.sbuf("modT", [128, c.L * 9 * DC], F32)
        self.sc1T = S.sbuf("sc1T", [128, c.L * 9 * DC], F32)
        self.lnT = S.sbuf("lnT", [128, c.L * 3 * 2 * DC], F32)
        self.identF = S.sbuf("identF", [128, 128], F32)
        self.onesF = S.sbuf("onesF", [128, 128], F32)
        self.onesB = S.sbuf("onesB", [128, 128], BF16)
        TH = c.TH
        self.xin = Rot([S.sbuf(f"xin{i}", [128, TH], F32) for i in range(2)])
        self.oslot = Rot([S.sbuf(f"osl{i}", [128, TH], F32) for i in range(2)])
        self.tmpA = Rot([S.sbuf(f"tmpA{i}", [128, TH], F32) for i in range(2)])
        self.tmpB = Rot([S.sbuf(f"tmpB{i}", [128, TH], F32) for i in range(2)])
        self.zsl = Rot([S.sbuf(f"zsl{i}", [128, TH], F32) for i in range(2)])
        self.mean = S.sbuf("mean", [128, TH], F32)
        self.rstd = S.sbuf("rstd", [128, TH], F32)
        self.nmr = S.sbuf("nmr", [128, TH], F32)
        self.var = S.sbuf("var", [128, TH], F32)
        self.bank = [S.psum(f"bank{i}", [128, 512], F32) for i in range(8)]
        reserve = 11 * 1024
        a0 = (nc.sbuf_base + 63) // 64 * 64
        asz = (nc.sbuf_top - a0 - reserve) // 64 * 64
        self._slab = nc.alloc_sbuf_tensor("arena_slab", [128, (a0 - nc.sbuf_base + asz) // 4], F32)
        self.arena0 = a0
        self.arena_end = a0 + asz
        nb = TH * 4
        ex = [self.at(f"xtra{i}", [128, TH], F32, (asz - (i + 1) * nb) // 64 * 64) for i in range(3)]
        self.xin = Rot(self.xin.bufs + [ex[0]])
        self.zsl = Rot(self.zsl.bufs + [ex[1], ex[2]])

    def reg(self, name):
        if name not in self.regions:
            self.regions[name] = self.S.region(name)
        return self.regions[name]

    def at(self, name, shape, dtype, off):
        nbytes = int(np.prod(shape[1:])) * (2 if dtype == BF16 else 4)
        assert self.arena0 + off + nbytes <= self.arena_end, (name, off, nbytes, self.arena_end - self.arena0)
        return self.S.sbuf_at(name, shape, dtype, self.arena0 + off)

    def mi(self, l, s, j, ch):
        return ((l * 3 + s) * 3 + j) * self.cfg.DC + ch

    def lni(self, l, s, gb, ch):
        return ((l * 3 + s) * 2 + gb) * self.cfg.DC + ch

    def phase_init(self):
        c, S, nc = self.cfg, self.S, self.nc
        DC = c.DC
        S.dma("sp", self.lnT[:], self.din["lnT"][:, :], self.lnT, writes=[self.lnT])
        S.op("dve", lambda h: h.memset(self.onesF[:], 1.0), writes=[self.onesF])
        S.op("dve", lambda h: h.memset(self.onesB[:], 1.0), writes=[self.onesB])
        idsrc = self.inp("ident", [128, 128])
        S.dma("sp", self.identF[:], idsrc[:, :], self.identF, writes=[self.identF])

    def phase_ada(self):
        c, S, nc = self.cfg, self.S, self.nc
        D, DC = c.D, c.DC
        cond = S.sbuf("cond", [128, DC], F32)
        scT = S.sbuf("scT", [128, DC], BF16)
        badaT = self.at("badaT", [128, c.L * 9 * DC], F32, 0)
        off = c.L * 9 * DC * 4
        off = (off + 63) // 64 * 64
        wsl = Rot([self.at(f"adaw{i}", [128, DC, 512], BF16, off + i * DC * 1024) for i in range(2)])
        rowb = Rot([S.sbuf(f"rowb{i}", [1, 512], F32) for i in range(1)])
        S.dma("sp", cond[:], self.din["cond"][:, :], cond, writes=[cond])
        S.dma("sp", badaT[:], self.din["b_adaT"][:, :], badaT, writes=[badaT])
        S.op("act", lambda h: h.activation(out=scT[:], in_=cond[:], func=AF.Silu), reads=[cond], writes=[scT])
        nblk = 9 * D // 512
        prow = self.bank[4]
        pmod = self.bank[5]
        for l in range(c.L):
            for cb in range(nblk):
                w = wsl.next()
                src = self.din["w_ada"][l * D:(l + 1) * D, cb * 512:(cb + 1) * 512].rearrange("(kc p) n -> p kc n", p=128)
                S.dma("pool", w[:], src, w, writes=[w])
                S.mm(prow, prow[0:1, 0:512], [(scT[:, kc:kc + 1], w[:, kc, :]) for kc in range(DC)], reads=[scT, w])
                rb = rowb.next()
                S.op("act", lambda h: h.activation(out=rb[:], in_=prow[0:1, 0:512], func=AF.Copy), reads=[prow], writes=[rb])
                for j in range(4):
                    S.mm(pmod, pmod[:, j:j + 1], [(rb[0:1, j * 128:(j + 1) * 128], self.onesF[0:1, 0:1])], reads=[rb, self.onesF], merge=(j > 0))
                base = l * 9 * DC + cb * 4
                S.op("dve", lambda h: h.tensor_tensor(out=self.modT[:, base:base + 4], in0=pmod[:, 0:4], in1=badaT[:, base:base + 4], op=ALU.add),
                     reads=[pmod, badaT], writes=[self.modT], merge=True)
        for l in range(c.L):
            for s in range(3):
                a = self.mi(l, s, 1, 0)
                S.op("dve", lambda h: h.tensor_scalar(out=self.sc1T[:, a:a + DC], in0=self.modT[:, a:a + DC], scalar1=1.0, scalar2=None, op0=ALU.add),
                     reads=[self.modT], writes=[self.sc1T], merge=True)
                g = self.mi(l, s, 2, 0)
                f = 1.0 if s == 1 else 0.5
                S.op("dve", lambda h: h.tensor_scalar(out=self.sc1T[:, g:g + DC], in0=self.modT[:, g:g + DC], scalar1=f, scalar2=None, op0=ALU.mult),
                     reads=[self.modT], writes=[self.sc1T], merge=True)
        S.barrier()

    def make_uT(self, l, s, Xin, uT, t0, ntok):
        c, S = self.cfg, self.S
        for ch in range(c.DC):
            for q0 in range(0, ntok, c.TH):
                xs = self.xin.next()
                S.dma("sp", xs[:], Xin[ch * 128:(ch + 1) * 128, t0 + q0:t0 + q0 + c.TH], xs,
                      reads=[self.reg(f"X{id(Xin)}")], writes=[xs])
                i_sc, i_sh = self.mi(l, s, 1, ch), self.mi(l, s, 0, ch)
                S.op("act", lambda h: h.activation(out=uT[:, ch, q0:q0 + c.TH], in_=xs[:], func=AF.Identity,
                                                   scale=self.sc1T[:, i_sc:i_sc + 1], bias=self.modT[:, i_sh:i_sh + 1]),
                     reads=[xs, self.sc1T, self.modT], writes=[uT], merge=True)

    def outproj_ln(self, l, s, hf, rhs_fn, rhs_bufs, nk, wsrc_row0, wsrc, wslots, Xin, Xout):
        c, S = self.cfg, self.S
        DC, TH, D = c.DC, c.TH, c.D
        t0 = hf * TH
        psY = [self.bank[4], self.bank[5]]
        psS1, psS2 = self.bank[6], self.bank[7]
        Xr_in, Xr_out, Zr = self.reg(f"X{id(Xin)}"), self.reg(f"X{id(Xout)}"), self.reg("Z")
        pend = None
        for dc in range(DC):
            w = wslots.next()
            S.dma("pool", w[:], wsrc[wsrc_row0 + dc * 128: wsrc_row0 + (dc + 1) * 128, :], w, writes=[w])
            py = psY[dc % 2]
            S.mm(py, py[:, 0:TH], [(w[:, k * 128:(k + 1) * 128], rhs_fn(k)) for k in range(nk)], reads=[w] + rhs_bufs)
            if pend is not None:
                self._stats_mm(*pend)
            xs = self.xin.next()
            S.dma("sp", xs[:], Xin[dc * 128:(dc + 1) * 128, t0:t0 + TH], xs, reads=[Xr_in], writes=[xs])
            t1 = self.tmpA.next()
            ig = self.mi(l, s, 2, dc)
            S.op("act", lambda h: h.activation(out=t1[:], in_=py[:, 0:TH], func=AF.Copy, scale=self.sc1T[:, ig:ig + 1]),
                 reads=[py, self.sc1T], writes=[t1])
            z = self.zsl.next()
            S.op("dve", lambda h: h.scalar_tensor_tensor(out=z[:], in0=xs[:], scalar=float(c.ALPHA), in1=t1[:], op0=ALU.mult, op1=ALU.add),
                 reads=[xs, t1], writes=[z])
            zq = self.tmpB.next()
            S.op("act", lambda h: h.activation(out=zq[:], in_=z[:], func=AF.Square), reads=[z], writes=[zq])
            S.dma("sp", self.Z[dc * 128:(dc + 1) * 128, t0:t0 + TH], z[:], z, reads=[z], writes=[Zr], merge=True)
            pend = (dc, z, zq, psS1, psS2)
        self._stats_mm(*pend)
        invD = 1.0 / D
        S.op("act", lambda h: h.activation(out=self.mean[:], in_=psS1[:, 0:TH], func=AF.Copy, scale=invD), reads=[psS1], writes=[self.mean])
        msq = self.tmpA.next()
        S.op("dve", lambda h: h.tensor_tensor(out=msq[:], in0=self.mean[:], in1=self.mean[:], op=ALU.mult), reads=[self.mean], writes=[msq])
        S.op("dve", lambda h: h.scalar_tensor_tensor(out=self.var[:], in0=psS2[:, 0:TH], scalar=invD, in1=msq[:], op0=ALU.mult, op1=ALU.subtract),
             reads=[psS2, msq], writes=[self.var])
        S.op("dve", lambda h: h.tensor_scalar(out=self.var[:], in0=self.var[:], scalar1=1e-5, scalar2=None, op0=ALU.add), reads=[self.var], writes=[self.var])
        S.op("act", lambda h: h.activation(out=self.var[:], in_=self.var[:], func=AF.Sqrt), reads=[self.var], writes=[self.var])
        S.op("dve", lambda h: h.reciprocal(out=self.rstd[:], in_=self.var[:]), reads=[self.var], writes=[self.rstd])
        S.op("dve", lambda h: h.scalar_tensor_tensor(out=self.nmr[:], in0=self.mean[:], scalar=-1.0, in1=self.rstd[:], op0=ALU.mult, op1=ALU.mult),
             reads=[self.mean, self.rstd], writes=[self.nmr])
        for dc in range(DC):
            z = self.zsl.next()
            S.dma("sp", z[:], self.Z[dc * 128:(dc + 1) * 128, t0:t0 + TH], z, reads=[Zr], writes=[z])
            t1 = self.tmpA.next()
            S.op("dve", lambda h: h.tensor_tensor(out=t1[:], in0=z[:], in1=self.rstd[:], op=ALU.mult), reads=[z, self.rstd], writes=[t1])
            t2 = self.tmpB.next()
            S.op("dve", lambda h: h.tensor_tensor(out=t2[:], in0=t1[:], in1=self.nmr[:], op=ALU.add), reads=[t1, self.nmr], writes=[t2])
            o = self.oslot.next()
            ig, ib = self.lni(l, s, 0, dc), self.lni(l, s, 1, dc)
            S.op("act", lambda h: h.activation(out=o[:], in_=t2[:], func=AF.Identity, scale=self.lnT[:, ig:ig + 1], bias=self.lnT[:, ib:ib + 1]),
                 reads=[t2, self.lnT], writes=[o])
            S.dma("sp", Xout[dc * 128:(dc + 1) * 128, t0:t0 + TH], o[:], o, reads=[o], writes=[Xr_out], merge=True)

    def _stats_mm(self, dc, z, zq, psS1, psS2):
        c, S = self.cfg, self.S
        TH, DC = c.TH, c.DC
        S.mm(psS1, psS1[:, 0:TH], [(self.onesF[:], z[:])], reads=[self.onesF, z], merge=(dc > 0), start=(dc == 0), stop=(dc == DC - 1))
        S.mm(psS2, psS2[:, 0:TH], [(self.onesF[:], zq[:])], reads=[self.onesF, zq], merge=(dc > 0), start=(dc == 0), stop=(dc == DC - 1))

    def phase_ffn(self, l, s, Xin, Xout):
        c, S = self.cfg, self.S
        DC, FC, TH = c.DC, c.FC, c.TH
        fi = 0 if s == 0 else 1
        uT = self.at("uT", [128, DC, TH], BF16, 0)
        hT = self.at("hT", [128, FC, TH], BF16, DC * TH * 2)
        woff = DC * TH * 2 + FC * TH * 2
        wgu = Rot([self.at(f"wgu{i}", [128, DC * 256], BF16, woff + i * DC * 512) for i in range(2)])
        wo = Rot([self.at("wo0", [128, FC * 128], BF16, 0), self.at("wo1", [128, FC * 128], BF16, woff)])
        sg = Rot([S.sbuf(f"sg{i}_{l}{s}", [128, TH], F32) for i in range(2)]) if not hasattr(self, "_sg") else self._sg
        self._sg = sg
        in_row0 = (l * 2 + fi) * FC * 128
        out_row0 = (l * 2 + fi) * DC * 128
        for hf in range(c.NH):
            self.make_uT(l, s, Xin, uT, hf * TH, TH)
            for f in range(FC):
                w = wgu.next()
                S.dma("pool", w[:], self.din["ffn_in"][in_row0 + f * 128: in_row0 + (f + 1) * 128, :], w, writes=[w])
                pg, pu = self.bank[(f % 2) * 2], self.bank[(f % 2) * 2 + 1]
                S.mm(pg, pg[:, 0:TH], [(w[:, kc * 256:kc * 256 + 128], uT[:, kc, :]) for kc in range(DC)], reads=[w, uT])
                S.mm(pu, pu[:, 0:TH], [(w[:, kc * 256 + 128:kc * 256 + 256], uT[:, kc, :]) for kc in range(DC)], reads=[w, uT])
                g = sg.next()
                S.op("act", lambda h: h.activation(out=g[:], in_=pg[:, 0:TH], func=AF.Silu), reads=[pg], writes=[g])
                S.op("dve", lambda h: h.tensor_tensor(out=hT[:, f, :], in0=pu[:, 0:TH], in1=g[:], op=ALU.mult), reads=[pu, g], writes=[hT], merge=True)
            S.barrier()
            self.outproj_ln(l, s, hf, lambda k: hT[:, k, :], [hT], FC, out_row0, self.din["ffn_out"], wo, Xin, Xout)
            S.barrier()


    def phase_even(self, l, Xin, Xout):
        c, S = self.cfg, self.S
        DC, TH, TT, GC, GW, NT = c.DC, c.TH, c.TT, c.GC, c.GW, c.NT
        MC = c.MIXW // 128
        PC = c.POOL_W // 128
        ei = l // 2
        if "ev_in" not in self.din:
            self.inp("ev_in", [((c.L + 1) // 2) * c.D, c.MIXW])
            self.inp("ev_out", [((c.L + 1) // 2) * DC * 128, MC * 128])
            self.inp("pool_w", [((c.L + 1) // 2) * 4 * GW, GW])
            self.inp("pool_scT", [128, ((c.L + 1) // 2) * PC])
            self.inp("poolM", [128, 4 * TT * 3 * 128])
            self.inp("cosN", [128, TT * NT])
            self.inp("sinN", [128, TT * NT])
            self.inp("cosC", [128, GC * GW])
            self.inp("nsinC", [128, GC * GW])
            self.pscT = S.sbuf("pscT", [128, ((c.L + 1) // 2) * PC], F32)
            S.dma("sp", self.pscT[:], self.din["pool_scT"][:, :], self.pscT, writes=[self.pscT])
        o_p = max(DC * TH * 2, 4 * GC * TH * 2 + 2 * GC * TH * 2 + 2 * GC * GW * 2, 2 * MC * 256)
        o_p = (o_p + 63) // 64 * 64
        o_mix = o_p + TT * c.MIXW * 2
        o_st = o_mix + MC * TH * 2
        uT = self.at("e_uT", [128, DC, TH], BF16, 0)
        pT = self.at("e_p", [128, TT, c.MIXW], BF16, o_p)
        mixT = self.at("e_mix", [128, MC, TH], BF16, o_mix)
        wblk = Rot([self.at(f"e_w{i}", [128, DC, 128], BF16, o_st + i * DC * 256) for i in range(2)])
        dT = self.at("e_dT", [128, 4 * GC, TH], BF16, 0)
        o1 = 4 * GC * TH * 2
        PcPs = self.at("e_pcps", [128, 2, GC, TH], BF16, o1)
        o2 = o1 + 2 * GC * TH * 2
        cosC = self.at("e_cosC", [128, GC, GW], BF16, o2)
        nsinC = self.at("e_nsinC", [128, GC, GW], BF16, o2 + GC * GW * 2)
        assert o2 + 2 * GC * GW * 2 <= o_p
        pm = self.at("e_pm", [128, TT, 3, 128], BF16, o_st)
        o3 = o_st + TT * 3 * 128 * 2
        pw = self.at("e_pw", [128, GC, GW], BF16, o3)
        o4 = o3 + GC * GW * 2
        cosN = self.at("e_cosN", [128, TT, TH], BF16, o4)
        sinN = self.at("e_sinN", [128, TT, TH], BF16, o4 + TT * TH * 2)
        wo = Rot([self.at(f"e_wo{i}", [128, MC * 128], BF16, i * MC * 256) for i in range(2)])
        assert 2 * MC * 256 <= o_p
        ev_in = self.din["ev_in"]
        tph = TH // 128
        for hf in range(c.NH):
            self.make_uT(l, 1, Xin, uT, hf * TH, TH)
            for cb in range(MC):
                w = wblk.next()
                src = ev_in[ei * c.D:(ei + 1) * c.D, cb * 128:(cb + 1) * 128].rearrange("(kc p) n -> p kc n", p=128)
                S.dma("pool", w[:], src, w, writes=[w])
                for tl in range(tph):
                    tt = hf * tph + tl
                    pb = self.bank[(cb * tph + tl) % 4]
                    S.mm(pb, pb[:, 0:128], [(uT[:, kc, tl * 128:(tl + 1) * 128], w[:, kc, :]) for kc in range(DC)], reads=[uT, w])
                    if (cb + tl) % 2 == 0:
                        S.op("act", lambda h: h.activation(out=pT[:, tt, cb * 128:(cb + 1) * 128], in_=pb[:, 0:128], func=AF.Copy), reads=[pb], writes=[pT], merge=True)
                    else:
                        S.op("dve", lambda h: h.tensor_copy(out=pT[:, tt, cb * 128:(cb + 1) * 128], in_=pb[:, 0:128]), reads=[pb], writes=[pT], merge=True)
        S.barrier()
        S.dma("pool", cosC[:], self.din["cosC"].rearrange("p (a b) -> p a b", a=GC), cosC, writes=[cosC])
        S.dma("pool", nsinC[:], self.din["nsinC"].rearrange("p (a b) -> p a b", a=GC), nsinC, writes=[nsinC])
        for hf in range(c.NH):
            t0 = hf * TH
            for g in range(4):
                S.dma("pool", pm[:], self.din["poolM"][:, g * TT * 384:(g + 1) * TT * 384].rearrange("p (a b c) -> p a b c", a=TT, b=3), pm, writes=[pm])
                S.dma("pool", pw[:], self.din["pool_w"][(ei * 4 + g) * GW:(ei * 4 + g + 1) * GW, :].rearrange("(a p) n -> p a n", p=128), pw, writes=[pw])
                for cc in range(GC):
                    col0 = g * GW + cc * 128
                    pb = self.bank[cc % 2]
                    for tl in range(tph):
                        j = hf * tph + tl
                        pairs = []
                        for nb in range(3):
                            i = j + nb - 1
                            if 0 <= i < TT:
                                pairs.append((pT[:, i, col0:col0 + 128], pm[:, j, nb, :]))
                        S.mm(pb, pb[:, tl * 128:(tl + 1) * 128], pairs, reads=[pT, pm], merge=(tl > 0))
                    S.op("act", lambda h: h.activation(out=dT[:, g * GC + cc, :], in_=pb[:, 0:TH], func=AF.Copy), reads=[pb], writes=[dT], merge=True)
                for d2 in range(GC):
                    pb = self.bank[2 + d2 % 2]
                    S.mm(pb, pb[:, 0:TH], [(pw[:, cc, d2 * 128:(d2 + 1) * 128], dT[:, g * GC + cc, :]) for cc in range(GC)], reads=[pw, dT])
                    isc = ei * PC + g * GC + d2
                    S.op("act", lambda h: h.activation(out=mixT[:, g * GC + d2, :], in_=pb[:, 0:TH], func=AF.Copy, scale=self.pscT[:, isc:isc + 1]),
                         reads=[pb, self.pscT], writes=[mixT], merge=True)
            S.dma("pool", cosN[:], self.din["cosN"].rearrange("p (a b) -> p a b", a=TT)[:, :, t0:t0 + TH], cosN, writes=[cosN])
            S.dma("pool", sinN[:], self.din["sinN"].rearrange("p (a b) -> p a b", a=TT)[:, :, t0:t0 + TH], sinN, writes=[sinN])
            for g in range(4):
                for cc in range(GC):
                    col0 = c.POOL_W + g * GW + cc * 128
                    for si, tab in enumerate((cosN, sinN)):
                        pb = self.bank[si]
                        S.mm(pb, pb[:, 0:TH], [(pT[:, i, col0:col0 + 128], tab[:, i, :]) for i in range(TT)], reads=[pT, tab])
                        if si == 0:
                            S.op("act", lambda h: h.activation(out=PcPs[:, si, cc, :], in_=pb[:, 0:TH], func=AF.Copy), reads=[pb], writes=[PcPs], merge=True)
                        else:
                            S.op("dve", lambda h: h.tensor_copy(out=PcPs[:, si, cc, :], in_=pb[:, 0:TH]), reads=[pb], writes=[PcPs], merge=True)
                for k2 in range(GC):
                    pb = self.bank[2 + k2 % 2]
                    pairs = [(cosC[:, cc, k2 * 128:(k2 + 1) * 128], PcPs[:, 0, cc, :]) for cc in range(GC)]
                    pairs += [(nsinC[:, cc, k2 * 128:(k2 + 1) * 128], PcPs[:, 1, cc, :]) for cc in range(GC)]
                    S.mm(pb, pb[:, 0:TH], pairs, reads=[cosC, nsinC, PcPs])
                    S.op("dve", lambda h: h.tensor_copy(out=mixT[:, PC + g * GC + k2, :], in_=pb[:, 0:TH]), reads=[pb], writes=[mixT], merge=True)
            S.barrier()
            self.outproj_ln(l, 1, hf, lambda k: mixT[:, k, :], [mixT], MC, ei * DC * 128, self.din["ev_out"], wo, Xin, Xout)
            S.barrier()
            if hf + 1 < c.NH:
                S.dma("pool", cosC[:], self.din["cosC"].rearrange("p (a b) -> p a b", a=GC), cosC, writes=[cosC])
                S.dma("pool", nsinC[:], self.din["nsinC"].rearrange("p (a b) -> p a b", a=GC), nsinC, writes=[nsinC])


    def odd_setup(self):
        c, S = self.cfg, self.S
        if hasattr(self, "_odd_ready"):
            return
        self._odd_ready = True
        NO = c.L // 2
        DC, NT, HC, MC = c.DC, c.NT, c.HC, c.MIXW // 128
        self.NFM = 3 * HC + c.GLC + 3 + c.NQ + c.NKV
        self.NRW = 3 * HC + c.GLC + 3
        for name, shape in (("od_in", [NO * self.NFM * 128, DC * 128]), ("od_inv", [NO * c.D, c.KVW]),
                            ("od_out", [NO * DC * 128, MC * 128]), ("muT", [128, NO * self.NRW]),
                            ("w0T", [128, NO * 2 * HC]), ("dw2", [NO * 2 * 128, c.RW]), ("a0T", [128, NO * HC]),
                            ("iw2", [NO * 128, c.RW]), ("gw2", [NO * c.GLC * 128, c.RW]), ("kkT", [128, NO * HC]),
                            ("kaT", [128, NO * HC]), ("rkT", [128, NO * HC]), ("gngT", [128, NO * HC]), ("gnbT", [128, NO * HC]),
                            ("qnT", [128, NO]), ("knT", [128, NO]),
                            ("mprev", [128, NT]), ("mnext", [128, NT]), ("ropeC", [128, NT]), ("ropeS", [128, NT]),
                            ("rotT", [128, 128]), ("Am", [8, NT]), ("Bm", [8, c.NKEY]),
                            ("cKT", [NO * c.NKV * 128, c.PAST]), ("cV", [NO * c.PAST, c.KVW]),
                            ("st0", [NO * 2 * 128, HC * 64]), ("mres", [128, 2 * c.NCH]),
                            ("masks", [128, 4 * 128]), ("bdones", [128, 128])):
            self.inp(name, shape)
        self.outp("kT_out", [NO * c.NKV * 128, NT])
        self.outp("v_out", [NO * NT, c.KVW])
        self.outp("st_out", [NO * 2 * c.NCH * 128, HC * 64])
        self.MIX = self.scratch("MIX", [MC * 128, NT])
        self.PR = self.scratch("PR", [3 * HC * 128, NT])
        nsm = NO * (self.NRW + 2 * HC + 6 * HC + 2)
        self.osm = S.sbuf("osm", [128, nsm + 2 * self.NRW * NO + 2 * c.NCH], F32)
        o = 0
        self.sm = {}
        for name, n in (("muT", NO * self.NRW), ("w0T", NO * 2 * HC), ("a0T", NO * HC), ("kkT", NO * HC), ("kaT", NO * HC),
                        ("rkT", NO * HC), ("gngT", NO * HC), ("gnbT", NO * HC), ("qnT", NO), ("knT", NO), ("mres", 2 * c.NCH)):
            self.sm[name] = o
            S.dma("sp", self.osm[:, o:o + n], self.din[name][:, :], self.osm, writes=[self.osm], merge=True)
            o += n
        self.sm["omm"] = o
        n = NO * self.NRW
        a = self.sm["muT"]
        S.op("dve", lambda h: h.tensor_scalar(out=self.osm[:, o:o + n], in0=self.osm[:, a:a + n], scalar1=-1.0, scalar2=1.0, op0=ALU.mult, op1=ALU.add),
             reads=[self.osm], writes=[self.osm])
        o2 = o + n
        self.sm["hmu"] = o2
        S.op("dve", lambda h: h.tensor_scalar(out=self.osm[:, o2:o2 + n], in0=self.osm[:, a:a + n], scalar1=0.5, scalar2=None, op0=ALU.mult),
             reads=[self.osm], writes=[self.osm])
        self.masks = S.sbuf("masks", [128, 4, 128], F32)
        S.dma("sp", self.masks[:], self.din["masks"].rearrange("p (a b) -> p a b", a=4), self.masks, writes=[self.masks])
        self.bdones = S.sbuf("bdones", [128, 128], F32)
        S.dma("sp", self.bdones[:], self.din["bdones"][:, :], self.bdones, writes=[self.bdones])
        self.rotT = S.sbuf("rotT", [128, 128], F32)
        S.dma("sp", self.rotT[:], self.din["rotT"][:, :], self.rotT, writes=[self.rotT])

    def smc(self, name, idx):
        o = self.sm[name] + idx
        return self.osm[:, o:o + 1]

    def phase_odd(self, l, Xin, Xout):
        c, S = self.cfg, self.S
        self.odd_setup()
        DC, NT, TH, HC, NH, MC = c.DC, c.NT, c.TH, c.HC, c.NH, c.MIXW // 128
        oi = l // 2
        NFM, NRW = self.NFM, self.NRW
        bump = [0]

        def A(name, shape, dtype):
            nb = int(np.prod(shape[1:])) * (2 if dtype == BF16 else 4)
            b = self.at("o_" + name, shape, dtype, bump[0])
            bump[0] = (bump[0] + nb + 63) // 64 * 64
            return b

        skip = getattr(c, "skip", ())
        if "pA" in skip:
            return
        uT = A("uT", [128, DC, NT], BF16)
        wfm = Rot([A(f"wfm{i}", [128, DC, 128], BF16) for i in range(2)])
        mprev = A("mprev", [128, NT], BF16)
        mnext = A("mnext", [128, NT], BF16)
        pf = A("pf", [128, NT + 2], F32)
        sgT = A("sgT", [128, c.GLC, NT], BF16)
        twT = A("twT", [128, 2, NT], BF16)
        alT = A("alT", [128, NT], BF16)
        base_after_shared = bump[0]
        T4 = Rot([A(f"T{i}", [128, NT], F32) for i in range(5)])
        S.dma("pool", mprev[:], self.din["mprev"][:, :], mprev, writes=[mprev])
        S.dma("pool", mnext[:], self.din["mnext"][:, :], mnext, writes=[mnext])
        S.op("dve", lambda h: h.memset(pf[:], 0.0), writes=[pf])
        self.make_uT(l, 1, Xin, uT, 0, NT)
        od_in = self.din["od_in"]

        def proj_fm(chunk, banks):
            w = wfm.next()
            r0 = (oi * NFM + chunk) * 128
            S.dma("pool", w[:], od_in[r0:r0 + 128, :].rearrange("p (a b) -> p a b", a=DC), w, writes=[w])
            for hf in range(NH):
                pb = banks[hf]
                S.mm(pb, pb[:, 0:TH], [(w[:, kc, :], uT[:, kc, hf * TH:(hf + 1) * TH]) for kc in range(DC)], reads=[w, uT])

        def shifted(chunk, banks, out_fn):
            for hf in range(NH):
                pb = banks[hf]
                S.op("act", lambda h: h.activation(out=pf[:, 1 + hf * TH:1 + (hf + 1) * TH], in_=pb[:, 0:TH], func=AF.Copy), reads=[pb], writes=[pf], merge=(hf > 0))
            t1, t2 = T4.next(), T4.next()
            S.op("dve", lambda h: h.tensor_tensor(out=t1[:], in0=pf[:, 0:NT], in1=mprev[:], op=ALU.mult), reads=[pf, mprev], writes=[t1])
            S.op("dve", lambda h: h.tensor_tensor(out=t2[:], in0=pf[:, 2:NT + 2], in1=mnext[:], op=ALU.mult), reads=[pf, mnext], writes=[t2])
            S.op("dve", lambda h: h.tensor_tensor(out=t1[:], in0=t1[:], in1=t2[:], op=ALU.add), reads=[t1, t2], writes=[t1])
            S.op("act", lambda h: h.activation(out=t2[:], in_=t1[:], func=AF.Copy, scale=self.smc("hmu", oi * NRW + chunk)), reads=[t1, self.osm], writes=[t2])
            S.op("dve", lambda h: h.scalar_tensor_tensor(out=t1[:], in0=pf[:, 1:NT + 1], scalar=self.smc("omm", oi * NRW + chunk), in1=t2[:], op0=ALU.mult, op1=ALU.add),
                 reads=[pf, t2, self.osm], writes=[t1])
            return t1

        pbA = [self.bank[4], self.bank[5]] if NH == 2 else [self.bank[4]]
        pbB = [self.bank[6], self.bank[7]] if NH == 2 else [self.bank[6]]
        if "pB" in skip:
            return
        cg0 = 3 * HC
        for gc in range(c.GLC):
            banks = pbA if gc % 2 == 0 else pbB
            proj_fm(cg0 + gc, banks)
            t = shifted(cg0 + gc, banks, None)
            S.op("act", lambda h: h.activation(out=sgT[:, gc, :], in_=t[:], func=AF.Sigmoid), reads=[t], writes=[sgT], merge=True)
        for d in range(2):
            banks = pbA if d % 2 == 0 else pbB
            proj_fm(cg0 + c.GLC + d, banks)
            t = shifted(cg0 + c.GLC + d, banks, None)
            S.op("act", lambda h: h.activation(out=twT[:, d, :], in_=t[:], func=AF.Tanh), reads=[t], writes=[twT], merge=True)
        proj_fm(cg0 + c.GLC + 2, pbA)
        t = shifted(cg0 + c.GLC + 2, pbA, None)
        S.op("act", lambda h: h.activation(out=alT[:], in_=t[:], func=AF.Copy), reads=[t], writes=[alT])
        if "pC" in skip:
            return
        PRr = self.reg("PR")
        for ch in range(3 * HC):
            banks = pbA if ch % 2 == 0 else pbB
            proj_fm(ch, banks)
            t = shifted(ch, banks, None)
            S.dma("sp", self.PR[ch * 128:(ch + 1) * 128, :], t[:], t, reads=[t], writes=[PRr], merge=True)
        if "pD" in skip:
            return
        if "att" not in getattr(c, "skip", ()):
            self.odd_attention(l, oi, uT, wfm, T4, proj_fm, A, pbA, pbB)
        S.barrier()
        if "rwkv" not in getattr(c, "skip", ()):
            self.odd_rwkv(l, oi, base_after_shared, sgT, twT, alT)
        S.barrier()
        if "pE" in skip:
            return
        mixT = self.at("o_mixT", [128, MC, TH], BF16, 0)
        wo = Rot([self.at(f"o_wo{i}", [128, MC * 128], BF16, MC * TH * 2 + i * MC * 256) for i in range(2)])
        MIXr = self.reg("MIX")
        for hf in range(NH):
            for k in range(MC):
                S.dma("pool", mixT[:, k, :], self.MIX[k * 128:(k + 1) * 128, hf * TH:(hf + 1) * TH], mixT, reads=[MIXr], writes=[mixT], merge=(k > 0))
            self.outproj_ln(l, 1, hf, lambda k: mixT[:, k, :], [mixT], MC, oi * DC * 128, self.din["od_out"], wo, Xin, Xout)
            S.barrier()

    def odd_attention(self, l, oi, uT, wfm, T4, proj_fm, A, pbA, pbB):
        c, S = self.cfg, self.S
        DC, NT, TH, HC, NH, KT = c.DC, c.NT, c.TH, c.HC, c.NH, c.KT
        PKT = c.PAST // 128
        NFM = self.NFM
        cq0 = 3 * HC + c.GLC + 3
        ck0 = cq0 + c.NQ
        ropeC = A("ropeC", [128, NT], F32)
        ropeS = A("ropeS", [128, NT], F32)
        Am = A("Am", [8, NT], BF16)
        Bm = A("Bm", [8, c.NKEY], BF16)
        KTg = A("KTg", [128, c.NKEY], BF16)
        Vg = A("Vg", [128, KT, 128], BF16)
        QT = A("QT", [128, NT], BF16)
        wv = A("wv", [128, DC, 128], BF16)
        vst = Rot([A(f"vst{i}", [128, 128], F32) for i in range(2)])
        Et = Rot([A(f"Et{i}", [128, TH], BF16) for i in range(3)])
        oT = Rot([A(f"oT{i}", [128, TH], F32) for i in range(2)])
        rden = A("rden", [128, TH], F32)
        S.dma("sp", ropeC[:], self.din["ropeC"][:, :], ropeC, writes=[ropeC])
        S.dma("sp", ropeS[:], self.din["ropeS"][:, :], ropeS, writes=[ropeS])
        S.dma("pool", Am[:], self.din["Am"][:, :], Am, writes=[Am])
        S.dma("pool", Bm[:], self.din["Bm"][:, :], Bm, writes=[Bm])
        MIXr = self.reg("MIX")
        scale = float(c.HD) ** -0.5

        def norm_rope(banks, gain_name, out_bf, out_col0, kout_rows=None):
            raw, sq, kn = T4.next(), T4.next(), T4.next()
            for hf in range(NH):
                sl = slice(hf * TH, (hf + 1) * TH)
                pb = banks[hf]
                S.op("act", lambda h: h.activation(out=raw[:, sl], in_=pb[:, 0:TH], func=AF.Copy), reads=[pb], writes=[raw], merge=(hf > 0))
                S.op("dve", lambda h: h.tensor_tensor(out=sq[:, sl], in0=pb[:, 0:TH], in1=raw[:, sl], op=ALU.mult), reads=[pb, raw], writes=[sq], merge=(hf > 0))
            for hf in range(NH):
                sl = slice(hf * TH, (hf + 1) * TH)
                pb = banks[hf]
                S.mm(pb, pb[:, 0:TH], [(self.onesF[:], sq[:, sl])], reads=[self.onesF, sq])
                S.op("act", lambda h: h.activation(out=sq[:, sl], in_=pb[:, 0:TH], func=AF.Sqrt, scale=1.0 / c.HD, bias=self.eps6[:, 0:1]), reads=[pb, self.eps6], writes=[sq])
            S.op("dve", lambda h: h.reciprocal(out=sq[:], in_=sq[:]), reads=[sq], writes=[sq])
            S.op("dve", lambda h: h.scalar_tensor_tensor(out=kn[:], in0=raw[:], scalar=self.smc(gain_name, oi), in1=sq[:], op0=ALU.mult, op1=ALU.mult),
                 reads=[raw, sq, self.osm], writes=[kn])
            if kout_rows is not None:
                S.dma("sp", self.dout["kT_out"][kout_rows:kout_rows + 128, :], kn[:], kn, reads=[kn])
            for hf in range(NH):
                sl = slice(hf * TH, (hf + 1) * TH)
                pb = banks[hf]
                S.mm(pb, pb[:, 0:TH], [(self.rotT[:], kn[:, sl])], reads=[self.rotT, kn])
                S.op("dve", lambda h: h.tensor_tensor(out=raw[:, sl], in0=pb[:, 0:TH], in1=ropeS[:, sl], op=ALU.mult), reads=[pb, ropeS], writes=[raw])
            S.op("dve", lambda h: h.tensor_tensor(out=sq[:], in0=kn[:], in1=ropeC[:], op=ALU.mult), reads=[kn, ropeC], writes=[sq])
            S.op("dve", lambda h: h.tensor_tensor(out=out_bf[:, out_col0:out_col0 + NT], in0=sq[:], in1=raw[:], op=ALU.add), reads=[sq, raw], writes=[out_bf])

        if not hasattr(self, "eps6"):
            self.eps6 = S.sbuf("eps6", [128, 2], F32)
            S.op("dve", lambda h: h.memset(self.eps6[:, 0:1], 1e-6), writes=[self.eps6])
            S.op("dve", lambda h: h.memset(self.eps6[:, 1:2], 1e-12), writes=[self.eps6])
        skip = getattr(c, "skip", ())
        if "a1" in skip:
            return
        for g in range(c.NKV):
            S.dma("pool", KTg[:, 0:c.PAST], self.din["cKT"][(oi * c.NKV + g) * 128:(oi * c.NKV + g + 1) * 128, :], KTg, writes=[KTg])
            proj_fm(ck0 + g, pbA)
            norm_rope(pbA, "knT", KTg, c.PAST, kout_rows=(oi * c.NKV + g) * 128)
            if "a2" in skip:
                continue
            if "a3b" not in skip:
              S.dma("pool", Vg[:, 0:PKT, :], self.din["cV"][oi * c.PAST:(oi + 1) * c.PAST, g * 128:(g + 1) * 128].rearrange("(a p) d -> p a d", p=128), Vg, writes=[Vg])
            S.dma("pool", wv[:], self.din["od_inv"][oi * c.D:(oi + 1) * c.D, g * 128:(g + 1) * 128].rearrange("(a p) d -> p a d", p=128), wv, writes=[wv])
            for tt in range(c.TT):
                if "a3c" in skip:
                    break
                pb = self.bank[tt % 2]
                S.mm(pb, pb[:, 0:128], [(uT[:, kc, tt * 128:(tt + 1) * 128], wv[:, kc, :]) for kc in range(DC)], reads=[uT, wv])
                vs = vst.next()
                if "a3d" not in skip:
                    S.op("act", lambda h: h.activation(out=vs[:], in_=pb[:, 0:128], func=AF.Copy), reads=[pb], writes=[vs])
                if "a3e" not in skip:
                    S.op("dve", lambda h: h.tensor_copy(out=Vg[:, PKT + tt, :], in_=pb[:, 0:128]), reads=[pb], writes=[Vg], merge=True)
                if "a3d" in skip:
                    continue
                if "a3a" not in skip:
                    S.dma("sp", self.dout["v_out"][oi * NT + tt * 128: oi * NT + (tt + 1) * 128, g * 128:(g + 1) * 128], vs[:], vs, reads=[vs])
            if "a3" in skip:
                continue
            for hq in range(c.GQA):
                h_ = g * c.GQA + hq
                proj_fm(cq0 + h_, pbB)
                norm_rope(pbB, "qnT", QT, 0)
                if "a4" in skip:
                    continue
                for hf in range(NH):
                    sl = slice(hf * TH, (hf + 1) * TH)
                    pO, pD = self.bank[2], self.bank[3]
                    pend = None
                    for kt in range(KT + 1):
                        if kt < KT:
                            ps_ = self.bank[kt % 2]
                            S.mm(ps_, ps_[:, 0:TH], [(KTg[:, kt * 128:(kt + 1) * 128], QT[:, sl]), (Bm[:, kt * 128:(kt + 1) * 128], Am[:, sl])], reads=[KTg, QT, Bm, Am])
                            e = Et.next()
                            S.op("act", lambda h: h.activation(out=e[:], in_=ps_[:, 0:TH], func=AF.Exp, scale=scale), reads=[ps_], writes=[e])
                        if pend is not None:
                            pk, pe_ = pend
                            S.mm(pO, pO[:, 0:TH], [(Vg[:, pk, :], pe_[:])], reads=[Vg, pe_], merge=(pk > 0), start=(pk == 0), stop=(pk == KT - 1))
                            S.mm(pD, pD[:, 0:TH], [(self.onesB[:], pe_[:])], reads=[self.onesB, pe_], merge=(pk > 0), start=(pk == 0), stop=(pk == KT - 1))
                        pend = (kt, e) if kt < KT else None
                    S.op("dve", lambda h: h.reciprocal(out=rden[:], in_=pD[:, 0:TH]), reads=[pD], writes=[rden])
                    o = oT.next()
                    S.op("dve", lambda h: h.tensor_tensor(out=o[:], in0=pO[:, 0:TH], in1=rden[:], op=ALU.mult), reads=[pO, rden], writes=[o])
                    r0 = (c.RW // 128 + h_) * 128
                    S.dma("sp", self.MIX[r0:r0 + 128, sl], o[:], o, reads=[o], writes=[MIXr], merge=True)


    def odd_rwkv(self, l, oi, base2, sgT, twT, alT):
        c, S = self.cfg, self.S
        DC, NT, TH, HC, NH, NCH, GLC = c.DC, c.NT, c.TH, c.HC, c.NH, c.NCH, c.GLC
        r1_end = DC * NT * 2
        st = {"o": 0, "second": False}

        def A(name, shape, dtype):
            nb = int(np.prod(shape[1:])) * (2 if dtype == BF16 else 4)
            nb = (nb + 63) // 64 * 64
            if not st["second"] and st["o"] + nb > r1_end:
                st["second"] = True
                st["o"] = max(base2, st["o"]) if st["o"] > r1_end else base2
            b = self.at("r_" + name, shape, dtype, st["o"])
            st["o"] += nb
            return b

        NEG = -math.exp(-0.5)
        v_ = A("v", [128, NT], F32)
        g_ = A("g", [128, NT], F32)
        yacc = A("yacc", [128, NT], F32)
        bonus = A("bonus", [128, NT], F32)
        At = [A(f"At{d}", [128, NT], F32) for d in range(2)]
        Rt = [A(f"Rt{d}", [128, NT], F32) for d in range(2)]
        Bt = [A(f"Bt{d}", [128, NT], F32) for d in range(2)]
        Kt = [A(f"Kt{d}", [128, NT], F32) for d in range(2)]
        pC = [A(f"pC{d}", [128, NCH], F32) for d in range(2)]
        ST = [A(f"ST{d}", [128, 64], F32) for d in range(2)]
        Send = [Rot([A(f"Send{d}{i}", [128, 64], F32) for i in range(2)]) for d in range(2)]
        dw2h = A("dw2h", [128, 2, 128], BF16)
        iw2h = A("iw2h", [128, 128], BF16)
        gw2h = A("gw2h", [128, GLC, 128], BF16)
        alias0 = dict(st)
        r_ = A("r", [128, NT], F32)
        k_ = A("k", [128, NT], F32)
        a_ = A("a", [128, NT], F32)
        kk_ = A("kk", [128, NT], F32)
        b_ = A("b", [128, NT], F32)
        lw = [A(f"lw{d}", [128, NT], F32) for d in range(2)]
        TP = Rot([A(f"tp{i}", [128, NT], F32) for i in range(5)])
        st.update(alias0)
        tm = [[A(f"tm{d}{i}", [128, 128], F32) for i in range(4)] for d in range(2)]
        IT = {}
        for d in range(2):
            for hp in range(2):
                IT[(d, hp)] = dict(
                    NT_=[A(f"PT{d}{hp}{i}", [128, 128], F32) for i in range(2)],
                    N=[A(f"P{d}{hp}{i}", [128, 128], F32) for i in range(2)],
                    MkT=A(f"MkT{d}{hp}", [128, 128], F32), QbT=A(f"QbT{d}{hp}", [128, 128], F32), QkT=A(f"QkT{d}{hp}", [128, 128], F32),
                    X=[A(f"X{d}{hp}{i}", [128, 128], F32) for i in range(2)])
        GT0s = [A(f"GT0s{d}", [128, 64], F32) for d in range(2)]
        H0p = [A(f"H0p{d}", [128, 64], F32) for d in range(2)]
        RhT = [A(f"RhT{d}", [128, 128], F32) for d in range(2)]
        PRr, MIXr = self.reg("PR"), self.reg("MIX")
        pbA = [self.bank[4], self.bank[5]] if NH == 2 else [self.bank[4]]
        pbB = [self.bank[6], self.bank[7]] if NH == 2 else [self.bank[6]]
        SU, SL, IU, IL = 0, 1, 2, 3
        mask_strict = [SU, SL]
        mask_incl = [IU, IL]

        def hsl(hf):
            return slice(hf * TH, (hf + 1) * TH)

        for hc in range(HC):
            S.dma("sp", r_[:], self.PR[hc * 128:(hc + 1) * 128, :], r_, reads=[PRr], writes=[r_])
            S.dma("sp", k_[:], self.PR[(HC + hc) * 128:(HC + hc + 1) * 128, :], k_, reads=[PRr], writes=[k_])
            S.dma("sp", v_[:], self.PR[(2 * HC + hc) * 128:(2 * HC + hc + 1) * 128, :], v_, reads=[PRr], writes=[v_])
            S.dma("pool", dw2h[:], self.din["dw2"][oi * 256:(oi + 1) * 256, hc * 128:(hc + 1) * 128].rearrange("(d p) n -> p d n", p=128), dw2h, writes=[dw2h])
            S.dma("pool", iw2h[:], self.din["iw2"][oi * 128:(oi + 1) * 128, hc * 128:(hc + 1) * 128], iw2h, writes=[iw2h])
            S.dma("pool", gw2h[:], self.din["gw2"][oi * GLC * 128:(oi + 1) * GLC * 128, hc * 128:(hc + 1) * 128].rearrange("(a p) n -> p a n", p=128), gw2h, writes=[gw2h])
            for d in range(2):
                S.dma("sp", ST[d][:], self.din["st0"][(oi * 2 + d) * 128:(oi * 2 + d + 1) * 128, hc * 64:(hc + 1) * 64], ST[d], writes=[ST[d]])
            S.op("dve", lambda h: h.memset(yacc[:], 0.0), writes=[yacc])
            for d in range(2):
                for hf in range(NH):
                    pb = pbA[hf]
                    S.mm(pb, pb[:, 0:TH], [(dw2h[:, d, :], twT[:, d, hsl(hf)])], reads=[dw2h, twT])
                    S.op("act", lambda h: h.activation(out=lw[d][:, hsl(hf)], in_=pb[:, 0:TH], func=AF.Sigmoid, bias=self.smc("w0T", (oi * 2 + d) * HC + hc)),
                         reads=[pb, self.osm], writes=[lw[d]], merge=(hf > 0))
                S.op("dve", lambda h: h.tensor_scalar(out=lw[d][:], in0=lw[d][:], scalar1=NEG, scalar2=None, op0=ALU.mult), reads=[lw[d]], writes=[lw[d]])
            for hf in range(NH):
                pb = pbB[hf]
                S.mm(pb, pb[:, 0:TH], [(iw2h[:], alT[:, hsl(hf)])], reads=[iw2h, alT])
                S.op("act", lambda h: h.activation(out=a_[:, hsl(hf)], in_=pb[:, 0:TH], func=AF.Sigmoid, bias=self.smc("a0T", oi * HC + hc)),
                     reads=[pb, self.osm], writes=[a_], merge=(hf > 0))
            for hf in range(NH):
                pb = pbA[hf]
                S.mm(pb, pb[:, 0:TH], [(gw2h[:, gc, :], sgT[:, gc, hsl(hf)]) for gc in range(GLC)], reads=[gw2h, sgT])
                S.op("act", lambda h: h.activation(out=g_[:, hsl(hf)], in_=pb[:, 0:TH], func=AF.Copy), reads=[pb], writes=[g_], merge=(hf > 0))
            t1, t2 = TP.next(), TP.next()
            S.op("act", lambda h: h.activation(out=t1[:], in_=k_[:], func=AF.Square, scale=self.smc("kkT", oi * HC + hc)), reads=[k_, self.osm], writes=[t1])
            for hf in range(NH):
                pb = pbB[hf]
                S.mm(pb, pb[:, 0:TH], [(self.bdones[:], t1[:, hsl(hf)])], reads=[self.bdones, t1])
                S.op("act", lambda h: h.activation(out=t2[:, hsl(hf)], in_=pb[:, 0:TH], func=AF.Sqrt, bias=self.eps6[:, 1:2]), reads=[pb, self.eps6], writes=[t2], merge=(hf > 0))
            S.op("dve", lambda h: h.reciprocal(out=t2[:], in_=t2[:]), reads=[t2], writes=[t2])
            S.op("dve", lambda h: h.scalar_tensor_tensor(out=kk_[:], in0=k_[:], scalar=self.smc("kkT", oi * HC + hc), in1=t2[:], op0=ALU.mult, op1=ALU.mult),
                 reads=[k_, t2, self.osm], writes=[kk_])
            S.op("dve", lambda h: h.tensor_scalar(out=t1[:], in0=a_[:], scalar1=-1.0, scalar2=self.smc("kaT", oi * HC + hc), op0=ALU.add, op1=ALU.mult),
                 reads=[a_, self.osm], writes=[t1])
            S.op("dve", lambda h: h.scalar_tensor_tensor(out=k_[:], in0=t1[:], scalar=1.0, in1=k_[:], op0=ALU.add, op1=ALU.mult), reads=[t1, k_], writes=[k_])
            S.op("dve", lambda h: h.tensor_tensor(out=b_[:], in0=kk_[:], in1=a_[:], op=ALU.mult), reads=[kk_, a_], writes=[b_])
            S.op("dve", lambda h: h.scalar_tensor_tensor(out=t1[:], in0=r_[:], scalar=self.smc("rkT", oi * HC + hc), in1=k_[:], op0=ALU.mult, op1=ALU.mult),
                 reads=[r_, k_, self.osm], writes=[t1])
            for hf in range(NH):
                pb = pbA[hf]
                S.mm(pb, pb[:, 0:TH], [(self.bdones[:], t1[:, hsl(hf)])], reads=[self.bdones, t1])
                S.op("dve", lambda h: h.tensor_tensor(out=bonus[:, hsl(hf)], in0=pb[:, 0:TH], in1=v_[:, hsl(hf)], op=ALU.mult), reads=[pb, v_], writes=[bonus], merge=(hf > 0))
            for d in range(2):
                pre, ex, inc = TP.next(), TP.next(), TP.next()
                for ch in range(NCH):
                    cs = slice(ch * 128, (ch + 1) * 128)
                    S.op("dve", lambda h: h.tensor_tensor_scan(out=pre[:, cs], data0=self.onesF[:], data1=lw[d][:, cs], initial=0.0, op0=ALU.mult, op1=ALU.add),
                         reads=[self.onesF, lw[d]], writes=[pre], merge=(ch > 0))
                if d == 0:
                    S.op("dve", lambda h: h.tensor_tensor(out=ex[:], in0=pre[:], in1=lw[d][:], op=ALU.subtract), reads=[pre, lw[d]], writes=[ex])
                    inc = pre
                else:
                    for ch in range(NCH):
                        cs = slice(ch * 128, (ch + 1) * 128)
                        S.op("dve", lambda h: h.tensor_scalar(out=ex[:, cs], in0=pre[:, cs], scalar1=pre[:, ch * 128 + 127:ch * 128 + 128], scalar2=-1.0, op0=ALU.subtract, op1=ALU.mult),
                             reads=[pre], writes=[ex], merge=(ch > 0))
                    S.op("dve", lambda h: h.tensor_tensor(out=inc[:], in0=ex[:], in1=lw[d][:], op=ALU.add), reads=[ex, lw[d]], writes=[inc])
                e = TP.next()
                S.op("act", lambda h: h.activation(out=e[:], in_=ex[:], func=AF.Exp), reads=[ex], writes=[e])
                S.op("dve", lambda h: h.scalar_tensor_tensor(out=At[d][:], in0=kk_[:], scalar=-1.0, in1=e[:], op0=ALU.mult, op1=ALU.mult), reads=[kk_, e], writes=[At[d]])
                e2 = TP.next()
                S.op("act", lambda h: h.activation(out=e2[:], in_=inc[:], func=AF.Exp), reads=[inc], writes=[e2])
                S.op("dve", lambda h: h.tensor_tensor(out=Rt[d][:], in0=r_[:], in1=e2[:], op=ALU.mult), reads=[r_, e2], writes=[Rt[d]])
                col = 127 if d == 0 else 0
                S.op("dve", lambda h: h.tensor_copy(out=pC[d][:], in_=e2[:, col:NT:128]), reads=[e2], writes=[pC[d]])
                S.op("act", lambda h: h.activation(out=e[:], in_=inc[:], func=AF.Exp, scale=-1.0), reads=[inc], writes=[e])
                S.op("dve", lambda h: h.tensor_tensor(out=Bt[d][:], in0=b_[:], in1=e[:], op=ALU.mult), reads=[b_, e], writes=[Bt[d]])
                S.op("dve", lambda h: h.tensor_tensor(out=Kt[d][:], in0=k_[:], in1=e[:], op=ALU.mult), reads=[k_, e], writes=[Kt[d]])
            S.barrier()
            for step in range(NCH):
                gens = [self._rwkv_item(oi, hc, d, hp, step, At, Rt, Bt, Kt, pC, v_, ST, Send, tm, IT[(d, hp)], GT0s, H0p, RhT, yacc, mask_strict, mask_incl)
                        for d in range(2) for hp in range(2)]
                while gens:
                    for gq in list(gens):
                        try:
                            next(gq)
                        except StopIteration:
                            gens.remove(gq)
            S.barrier()
            for hf in range(NH):
                sl = hsl(hf)
                p1, p2 = pbA[hf], pbB[hf]
                ysq, mean, tt_ = TP.next(), TP.next(), TP.next()
                S.op("act", lambda h: h.activation(out=ysq[:, sl], in_=yacc[:, sl], func=AF.Square), reads=[yacc], writes=[ysq])
                S.mm(p1, p1[:, 0:TH], [(self.bdones[:], yacc[:, sl])], reads=[self.bdones, yacc])
                S.mm(p2, p2[:, 0:TH], [(self.bdones[:], ysq[:, sl])], reads=[self.bdones, ysq])
                S.op("act", lambda h: h.activation(out=mean[:, sl], in_=p1[:, 0:TH], func=AF.Copy, scale=1.0 / 64), reads=[p1], writes=[mean])
                S.op("dve", lambda h: h.tensor_tensor(out=ysq[:, sl], in0=mean[:, sl], in1=mean[:, sl], op=ALU.mult), reads=[mean], writes=[ysq])
                S.op("dve", lambda h: h.scalar_tensor_tensor(out=tt_[:, sl], in0=p2[:, 0:TH], scalar=1.0 / 64, in1=ysq[:, sl], op0=ALU.mult, op1=ALU.subtract), reads=[p2, ysq], writes=[tt_])
                S.op("dve", lambda h: h.tensor_scalar(out=tt_[:, sl], in0=tt_[:, sl], scalar1=64e-5, scalar2=None, op0=ALU.add), reads=[tt_], writes=[tt_])
                S.op("act", lambda h: h.activation(out=tt_[:, sl], in_=tt_[:, sl], func=AF.Sqrt), reads=[tt_], writes=[tt_])
                S.op("dve", lambda h: h.reciprocal(out=tt_[:, sl], in_=tt_[:, sl]), reads=[tt_], writes=[tt_])
                S.op("dve", lambda h: h.tensor_tensor(out=ysq[:, sl], in0=yacc[:, sl], in1=mean[:, sl], op=ALU.subtract), reads=[yacc, mean], writes=[ysq])
                S.op("dve", lambda h: h.tensor_tensor(out=ysq[:, sl], in0=ysq[:, sl], in1=tt_[:, sl], op=ALU.mult), reads=[ysq, tt_], writes=[ysq])
                S.op("act", lambda h: h.activation(out=ysq[:, sl], in_=ysq[:, sl], func=AF.Identity, scale=self.smc("gngT", oi * HC + hc), bias=self.smc("gnbT", oi * HC + hc)),
                     reads=[ysq, self.osm], writes=[ysq])
                S.op("dve", lambda h: h.tensor_tensor(out=ysq[:, sl], in0=ysq[:, sl], in1=bonus[:, sl], op=ALU.add), reads=[ysq, bonus], writes=[ysq])
                ob = self.oslot.next()
                S.op("dve", lambda h: h.tensor_tensor(out=ob[:], in0=ysq[:, sl], in1=g_[:, sl], op=ALU.mult), reads=[ysq, g_], writes=[ob])
                S.dma("sp", self.MIX[hc * 128:(hc + 1) * 128, sl], ob[:], ob, reads=[ob], writes=[MIXr], merge=True)
            S.barrier()

    def _rwkv_item(self, oi, hc, d, hp, step, At, Rt, Bt, Kt, pC, v_, ST, Send, tm, it, GT0s, H0p, RhT, yacc, mask_strict, mask_incl):
        c, S = self.cfg, self.S
        NCH, HC = c.NCH, c.HC
        ch = step if d == 0 else NCH - 1 - step
        cs = slice(ch * 128, (ch + 1) * 128)
        ps = slice(hp * 64, hp * 64 + 64)
        bk = self.bank[d * 2 + hp]
        q = [bk[:, i * 128:(i + 1) * 128] for i in range(4)]
        At_tm, Bt_tm, Kt_tm, V_tm = tm[d]
        ms, mi_ = self.masks[:, mask_strict[d], :], self.masks[:, mask_incl[d], :]
        ms_t = self.masks[:, mask_strict[1 - d], :]
        if hp == 0:
            tb = self.bank[4 + d]
            for i, src in enumerate((At[d], Bt[d], Kt[d], v_)):
                S.mm(tb, tb[:, i * 128:(i + 1) * 128], [(src[:, cs], self.identF[:])], reads=[src, self.identF], transpose=True, merge=(i > 0))
            for i, dst in enumerate(tm[d]):
                if i % 2 == 0:
                    S.op("act", lambda h: h.activation(out=dst[:], in_=tb[:, i * 128:(i + 1) * 128], func=AF.Copy), reads=[tb], writes=[dst])
                else:
                    S.op("dve", lambda h: h.tensor_copy(out=dst[:], in_=tb[:, i * 128:(i + 1) * 128]), reads=[tb], writes=[dst])
        yield
        PT, P, X = it["NT_"], it["N"], it["X"]
        a_ps, b_ps, k_ps, r_ps = At[d][ps, cs], Bt[d][ps, cs], Kt[d][ps, cs], Rt[d][ps, cs]
        S.mm(bk, q[0], [(b_ps, a_ps)], reads=[Bt[d], At[d]])
        S.mm(bk, q[1], [(a_ps, b_ps)], reads=[Bt[d], At[d]], merge=True)
        S.mm(bk, q[2], [(k_ps, a_ps)], reads=[Kt[d], At[d]], merge=True)
        S.mm(bk, q[3], [(b_ps, r_ps)], reads=[Bt[d], Rt[d]], merge=True)
        S.op("dve", lambda h: h.tensor_tensor(out=PT[0][:], in0=q[0], in1=ms, op=ALU.mult), reads=[bk, self.masks], writes=[PT[0]])
        S.op("dve", lambda h: h.tensor_tensor(out=P[0][:], in0=q[1], in1=ms_t, op=ALU.mult), reads=[bk, self.masks], writes=[P[0]])
        S.op("dve", lambda h: h.tensor_tensor(out=it["MkT"][:], in0=q[2], in1=ms, op=ALU.mult), reads=[bk, self.masks], writes=[it["MkT"]])
        S.op("dve", lambda h: h.tensor_tensor(out=it["QbT"][:], in0=q[3], in1=mi_, op=ALU.mult), reads=[bk, self.masks], writes=[it["QbT"]])
        yield
        S.mm(bk, q[0], [(k_ps, r_ps)], reads=[Kt[d], Rt[d]])
        S.mm(bk, q[1][:, 0:64], [(it["MkT"][:], V_tm[:, ps])], reads=[it["MkT"], V_tm], merge=True)
        S.op("dve", lambda h: h.tensor_tensor(out=it["QkT"][:], in0=q[0], in1=mi_, op=ALU.mult), reads=[bk, self.masks], writes=[it["QkT"]])
        S.op("act", lambda h: h.activation(out=X[0][:, 0:64], in_=At_tm[:, ps], func=AF.Copy), reads=[At_tm], writes=[X[0]])
        S.op("dve", lambda h: h.tensor_copy(out=X[0][:, 64:128], in_=q[1][:, 0:64]), reads=[bk], writes=[X[0]], merge=True)
        yield
        cur = 0
        for lev in range(7):
            nxt = 1 - cur
            S.mm(bk, q[0], [(PT[cur][:], X[cur][:])], reads=[PT[cur], X[cur]])
            if lev < 6:
                S.mm(bk, q[1], [(PT[cur][:], P[cur][:])], reads=[PT[cur], P[cur]], merge=True)
                S.mm(bk, q[2], [(P[cur][:], PT[cur][:])], reads=[PT[cur], P[cur]], merge=True)
            S.op("dve", lambda h: h.tensor_tensor(out=X[nxt][:], in0=q[0], in1=X[cur][:], op=ALU.add), reads=[bk, X[cur]], writes=[X[nxt]])
            if lev < 6:
                S.op("act", lambda h: h.activation(out=P[nxt][:], in_=q[1], func=AF.Copy), reads=[bk], writes=[P[nxt]])
                S.op("act", lambda h: h.activation(out=PT[nxt][:], in_=q[2], func=AF.Copy), reads=[bk], writes=[PT[nxt]])
            cur = nxt
            yield
        Xf = X[cur]
        ci = ch
        S.mm(bk, bk[ps, 0:64], [(Xf[:, 0:64], Bt_tm[:, ps])], reads=[Xf, Bt_tm])
        S.mm(bk, bk[ps, 64:128], [(Bt_tm[:, ps], Xf[:, 64:128]), (Kt_tm[:, ps], V_tm[:, ps])], reads=[Xf, Bt_tm, Kt_tm, V_tm], merge=True)
        S.mm(bk, bk[ps, 128:256], [(Xf[:, 0:64], it["QbT"][:])], reads=[Xf, it["QbT"]], merge=True)
        S.op("dve", lambda h: h.tensor_tensor(out=GT0s[d][ps, :], in0=bk[ps, 0:64], in1=self.identF[ps, ps], op=ALU.add), reads=[bk, self.identF], writes=[GT0s[d]], merge=True)
        S.op("dve", lambda h: h.tensor_scalar(out=H0p[d][ps, :], in0=bk[ps, 64:128], scalar1=pC[d][ps, ci:ci + 1], scalar2=None, op0=ALU.mult), reads=[bk, pC[d]], writes=[H0p[d]], merge=True)
        S.op("dve", lambda h: h.tensor_tensor(out=RhT[d][ps, :], in0=bk[ps, 128:256], in1=Rt[d][ps, cs], op=ALU.add), reads=[bk, Rt[d]], writes=[RhT[d]], merge=True)
        yield
        yb = self.bank[6 + d]
        S.mm(yb, yb[ps, 128:256], [(Xf[:, 64:128], it["QbT"][:]), (V_tm[:, ps], it["QkT"][:]), (ST[d][ps, :], RhT[d][ps, :])],
             reads=[Xf, it["QbT"], V_tm, it["QkT"], ST[d], RhT[d]], merge=(hp > 0))
        S.mm(yb, yb[ps, 0:64], [(GT0s[d][ps, :], ST[d][ps, :])], reads=[GT0s[d], ST[d]], merge=True)
        yield
        if hp == 1:
            S.op("dve", lambda h: h.tensor_tensor(out=yacc[:, cs], in0=yb[:, 128:256], in1=yacc[:, cs], op=ALU.add), reads=[yb, yacc], writes=[yacc])
            se = Send[d].next()
            S.op("dve", lambda h: h.scalar_tensor_tensor(out=se[:], in0=yb[:, 0:64], scalar=pC[d][:, ci:ci + 1], in1=H0p[d][:], op0=ALU.mult, op1=ALU.add),
                 reads=[yb, pC[d], H0p[d]], writes=[se])
            if step + 1 < NCH:
                S.op("dve", lambda h: h.tensor_scalar(out=ST[d][:], in0=se[:], scalar1=self.smc("mres", d * NCH + step), scalar2=None, op0=ALU.mult),
                     reads=[se, self.osm], writes=[ST[d]])
            r0 = ((oi * 2 + d) * NCH + ch) * 128
            S.dma("sp", self.dout["st_out"][r0:r0 + 128, hc * 64:(hc + 1) * 64], se[:], se, reads=[se])

    def dump(self, name, buf, shape):
        o = self.outp(name, shape)
        self.S.dma("sp", o[:, :], buf[:], buf, reads=[buf])

    def build(self):
        c, S = self.cfg, self.S
        self.phase_init()
        self.phase_ada()
        if getattr(c, "debug", False):
            self.dump("dbg_mod", self.modT, [128, c.L * 9 * c.DC])
            self.dump("dbg_sc1", self.sc1T, [128, c.L * 9 * c.DC])
        k = 0
        done = False
        for l in range(c.L):
            for s in range(3):
                if c.stop_after is not None and k >= c.stop_after:
                    done = True
                    break
                Xin, Xout = self.X[k], self.X[k + 1]
                if s != 1:
                    self.phase_ffn(l, s, Xin, Xout)
                elif l % 2 == 0:
                    self.phase_even(l, Xin, Xout)
                else:
                    self.phase_odd(l, Xin, Xout)
                k += 1
            if done:
                break
        if k < c.L * 3:
            src = self.X[k]
            S.barrier()
            for ch in range(c.DC):
                o = self.oslot.next()
                S.dma("sp", o[:, 0:c.TH], src[ch * 128:(ch + 1) * 128, 0:c.TH], o, writes=[o])
                S.dma("sp", self.dout["yT"][ch * 128:(ch + 1) * 128, 0:c.TH], o[:, 0:c.TH], o, reads=[o])
                for hf in range(1, c.NH):
                    o = self.oslot.next()
                    S.dma("sp", o[:, 0:c.TH], src[ch * 128:(ch + 1) * 128, hf * c.TH:(hf + 1) * c.TH], o, writes=[o])
                    S.dma("sp", self.dout["yT"][ch * 128:(ch + 1) * 128, hf * c.TH:(hf + 1) * c.TH], o[:, 0:c.TH], o, reads=[o])
        S.finish()
        return self.nc


def tlay(v, nch=None):
    v = np.asarray(v, np.float32).reshape(-1, 128)
    return np.ascontiguousarray(v.T)


def prep_shared(cfg, inp):
    c = cfg
    D, DC, FC, L = c.D, c.DC, c.FC, c.L
    sh = {}
    sh["w_ada"] = np.ascontiguousarray(inp["w_ada"].reshape(L * D, 9 * D))
    sh["b_adaT"] = np.concatenate([tlay(inp["b_ada"][l]) for l in range(L)], axis=1)
    sh["lnT"] = np.concatenate([tlay(inp[k][l, s]) for l in range(L) for s in range(3) for k in ("ln_g", "ln_b")], axis=1)
    w = inp["ffn_w_in"].reshape(L * 2, DC, 128, 2, FC, 128)
    sh["ffn_in"] = np.ascontiguousarray(w.transpose(0, 4, 2, 1, 3, 5)).reshape(L * 2 * FC * 128, DC * 256)
    w = inp["ffn_w_out"].reshape(L * 2, FC, 128, DC, 128)
    sh["ffn_out"] = np.ascontiguousarray(w.transpose(0, 3, 2, 1, 4)).reshape(L * 2 * DC * 128, FC * 128)
    sh["ident"] = np.eye(128, dtype=np.float32)
    NE = (L + 1) // 2
    MC = c.MIXW // 128
    sh["ev_in"] = np.ascontiguousarray(inp["even_w_in"].reshape(NE * D, c.MIXW))
    w = inp["even_w_out"].reshape(NE, MC, 128, DC, 128)
    sh["ev_out"] = np.ascontiguousarray(w.transpose(0, 3, 2, 1, 4)).reshape(NE * DC * 128, MC * 128)
    sh["pool_w"] = np.ascontiguousarray(inp["pool_w"].reshape(NE * 4 * c.GW, c.GW))
    sh["pool_scT"] = np.concatenate([tlay(inp["pool_scale"][i]) for i in range(NE)], axis=1)
    kc = np.arange(c.GW, dtype=np.float64)
    ang = 2 * np.pi * np.outer(kc, kc) / c.GW
    sh["cosC"] = np.ascontiguousarray(np.cos(ang).reshape(c.GC, 128, c.GW).transpose(1, 0, 2)).reshape(128, -1).astype(np.float32)
    sh["nsinC"] = np.ascontiguousarray((-np.sin(ang)).reshape(c.GC, 128, c.GW).transpose(1, 0, 2)).reshape(128, -1).astype(np.float32)
    prep_shared_odd(cfg, inp, sh)
    return sh


def prep_shared_odd(cfg, inp, sh):
    c = cfg
    NO = c.L // 2
    if NO == 0:
        return
    D, DC, HC, RW, GL = c.D, c.DC, c.HC, c.RW, c.GATE_LORA
    MC = c.MIXW // 128
    od_in, od_inv, muT = [], [], []
    for oi in range(NO):
        W = inp["odd_w_in"][oi]
        mu = inp["shift_mu"][oi]
        ranges = [(i * 128, 128) for i in range(3 * HC)]
        g0 = 3 * RW
        for gc in range(c.GLC):
            ranges.append((g0 + gc * 128, min(128, GL - gc * 128)))
        w0 = g0 + GL
        ranges += [(w0, 128), (w0 + 128, 128), (w0 + 256, 128)]
        nrw = len(ranges)
        q0 = c.RCOLS
        ranges += [(q0 + i * 128, 128) for i in range(c.NQ)]
        k0 = q0 + c.ATT_W
        ranges += [(k0 + i * 128, 128) for i in range(c.NKV)]
        blocks = np.zeros((len(ranges), 128, DC, 128), np.float32)
        mus = np.zeros((128, nrw), np.float32)
        for ci, (c0, wd) in enumerate(ranges):
            blk = W[:, c0:c0 + wd].reshape(DC, 128, wd)
            blocks[ci, :, :, :wd] = blk.transpose(1, 0, 2)
            if ci < nrw:
                mus[:wd, ci] = mu[c0:c0 + wd]
        od_in.append(blocks.reshape(len(ranges) * 128, DC * 128))
        od_inv.append(np.ascontiguousarray(W[:, k0 + c.KVW:k0 + 2 * c.KVW]))
        muT.append(mus)
    sh["od_in"] = np.concatenate(od_in, 0)
    sh["od_inv"] = np.concatenate(od_inv, 0)
    sh["muT"] = np.concatenate(muT, 1)
    w = inp["odd_w_out"].reshape(NO, MC, 128, DC, 128)
    sh["od_out"] = np.ascontiguousarray(w.transpose(0, 3, 2, 1, 4)).reshape(NO * DC * 128, MC * 128)
    sh["w0T"] = np.concatenate([tlay(inp["decay_w0"][oi, d]) for oi in range(NO) for d in range(2)], 1)
    sh["dw2"] = np.ascontiguousarray(inp["decay_w2"].reshape(NO * 2 * 128, RW))
    sh["a0T"] = np.concatenate([tlay(inp["iclr_a0"][oi]) for oi in range(NO)], 1)
    sh["iw2"] = np.ascontiguousarray(inp["iclr_w2"].reshape(NO * 128, RW))
    gw = np.zeros((NO, c.GLC * 128, RW), np.float32)
    gw[:, :GL] = inp["gate_w2"]
    sh["gw2"] = gw.reshape(NO * c.GLC * 128, RW)
    for nm, key in (("kkT", "k_k"), ("kaT", "k_a"), ("gngT", "gn_g"), ("gnbT", "gn_b")):
        sh[nm] = np.concatenate([tlay(inp[key][oi]) for oi in range(NO)], 1)
    sh["rkT"] = np.concatenate([tlay(inp["r_k"][oi].reshape(-1)) for oi in range(NO)], 1)
    sh["qnT"] = np.concatenate([tlay(inp["q_norm"][oi]) for oi in range(NO)], 1)
    sh["knT"] = np.concatenate([tlay(inp["k_norm"][oi]) for oi in range(NO)], 1)
    r = np.arange(128)
    M = np.stack([(r[:, None] < r[None, :]), (r[:, None] > r[None, :]), (r[:, None] <= r[None, :]), (r[:, None] >= r[None, :])], 1).astype(np.float32)
    sh["masks"] = np.ascontiguousarray(M).reshape(128, 4 * 128)
    sh["bdones"] = ((r[:, None] // 64) == (r[None, :] // 64)).astype(np.float32)
    HD = c.HD
    half, quarter = HD // 2, HD // 4
    P = np.zeros((HD, HD), np.float32)
    for dd in range(HD):
        if dd % half < quarter:
            P[dd, dd + quarter] = -1.0
        else:
            P[dd, dd - quarter] = 1.0
    sh["rotT"] = np.ascontiguousarray(P.T)


def core_consts_odd(cfg, inp, ci, is_s, m):
    c = cfg
    NO = c.L // 2
    if NO == 0:
        return
    NT, HD, NCH, HC = c.NT, c.HD, c.NCH, c.HC
    nseq, slen = seq_struct(c, is_s)
    t = np.arange(NT)
    sid, pos = t // slen, t % slen
    m["mprev"] = np.ascontiguousarray(np.broadcast_to((pos > 0).astype(np.float32), (128, NT)))
    m["mnext"] = np.ascontiguousarray(np.broadcast_to((pos < slen - 1).astype(np.float32), (128, NT)))
    half, quarter = HD // 2, HD // 4
    C = np.ones((HD, NT), np.float32)
    Sn = np.zeros((HD, NT), np.float32)
    if is_s:
        inv = (10000.0 ** (-np.arange(quarter, dtype=np.float32) / quarter)).astype(np.float32)
        row = (t // c.GRID_W).astype(np.float32)
        col = (t % c.GRID_W).astype(np.float32)
        for dd in range(HD):
            p_ = row if dd // half == 0 else col
            ang = (p_ * inv[dd % quarter]).astype(np.float32)
            C[dd] = np.cos(ang)
            Sn[dd] = np.sin(ang)
    m["ropeC"], m["ropeS"] = C, Sn
    BIG = 8192.0
    Am = np.zeros((8, NT), np.float32)
    Bm = np.zeros((8, c.NKEY), np.float32)
    for r in range(nseq):
        Am[r, sid == r] = BIG
        Bm[r, c.PAST:][sid == r] = 1.0
    if is_s:
        Bm[0, :c.PAST] = 1.0
    Am[7] = -BIG
    Bm[7] = 1.0
    m["Am"], m["Bm"] = Am, Bm
    cKT = np.zeros((NO * c.NKV * 128, c.PAST), np.float32)
    cV = np.zeros((NO * c.PAST, c.KVW), np.float32)
    st0 = np.zeros((NO * 2 * 128, HC * 64), np.float32)
    if is_s:
        for oi in range(NO):
            cKT[oi * c.NKV * 128:(oi + 1) * c.NKV * 128] = inp["cache_k"][ci, oi].transpose(1, 2, 0).reshape(c.NKV * 128, c.PAST)
            cV[oi * c.PAST:(oi + 1) * c.PAST] = inp["cache_v"][ci, oi].reshape(c.PAST, c.KVW)
            for d in range(2):
                a = inp["state_rwkv"][ci, oi, d].reshape(HC, 2, 64, 64)
                st0[(oi * 2 + d) * 128:(oi * 2 + d + 1) * 128] = a.transpose(1, 3, 0, 2).reshape(128, HC * 64)
    m["cKT"], m["cV"], m["st0"] = cKT, cV, st0
    mres = np.ones((128, 2 * NCH), np.float32)
    for k in range(NCH):
        if ((k + 1) * 128) % slen == 0:
            mres[:, k] = 0.0
        ch = NCH - 1 - k
        if (ch * 128) % slen == 0:
            mres[:, NCH + k] = 0.0
    m["mres"] = mres


def seq_struct(cfg, is_s):
    c = cfg
    slen = c.NT if is_s else c.SEQ
    return c.NT // slen, slen


def core_consts(cfg, is_s):
    c = cfg
    NT, TT = c.NT, c.TT
    nseq, slen = seq_struct(c, is_s)
    m = {}
    t = np.arange(NT)
    sid = t // slen
    pos = t % slen
    same = sid[:, None] == sid[None, :]
    pm = np.zeros((4, 128, TT, 3, 128), np.float32)
    for wi, w in enumerate(c.POOL_WINDOWS):
        lo = np.clip(pos - w // 2, 0, slen) + sid * slen
        hi = np.clip(pos + w // 2, 0, slen) + sid * slen
        M = ((t[None, :] >= lo[:, None]) & (t[None, :] < hi[:, None])).astype(np.float64) / (hi - lo)[:, None]
        M = M - np.eye(NT)
        for j in range(TT):
            for nb in range(3):
                i = j + nb - 1
                if 0 <= i < TT:
                    pm[wi, :, j, nb, :] = M[j * 128:(j + 1) * 128, i * 128:(i + 1) * 128].T
    m["poolM"] = np.ascontiguousarray(pm.transpose(1, 0, 2, 3, 4)).reshape(128, -1)
    ang = 2 * np.pi * np.outer(pos, pos).astype(np.float64) / slen
    sc = (slen * c.GW) ** -0.5
    CN = np.where(same, np.cos(ang), 0.0) * sc
    SN = np.where(same, np.sin(ang), 0.0) * sc
    m["cosN"] = np.ascontiguousarray(CN.reshape(TT, 128, NT).transpose(1, 0, 2)).reshape(128, -1).astype(np.float32)
    m["sinN"] = np.ascontiguousarray(SN.reshape(TT, 128, NT).transpose(1, 0, 2)).reshape(128, -1).astype(np.float32)
    return m


def core_tokens(cfg, inp, ci):
    c = cfg
    if ci < c.NSAMP:
        return inp["x_sample"][ci], inp["c"][ci], True
    pc = ci - c.NSAMP
    nps = c.NT // c.SEQ
    return inp["x_prompt"][pc * nps:(pc + 1) * nps].reshape(c.NT, c.D), inp["c_ctx"], False


def prep_core(cfg, inp, ci, sh):
    x, cond, is_s = core_tokens(cfg, inp, ci)
    m = dict(sh)
    m["xT"] = np.ascontiguousarray(x.T)
    m["cond"] = tlay(cond)
    m.update(core_consts(cfg, is_s))
    core_consts_odd(cfg, inp, ci, is_s, m)
    return m


def run_cfg(cfg, inputs):
    c = cfg
    inp = {k: np.asarray(v) for k, v in inputs.items()}
    prog = Program(c)
    nc = prog.build()
    sh = prep_shared(c, inp)
    ncores = c.NSAMP + c.NPROMPT_CORES
    maps = []
    for ci in range(ncores):
        m = prep_core(c, inp, ci, sh)
        mm = {}
        for k, ap in prog.din.items():
            a = np.asarray(m[k], dtype=np.float32)
            shp = tuple(int(x) for x in ap.shape)
            if a.shape != shp:
                a = a.reshape(shp)
            mm[k] = np.ascontiguousarray(a)
        maps.append(mm)
    res = run_bass_kernel_spmd(nc, maps, core_ids=list(range(ncores))).results
    NO, NT, D, SEQ = c.L // 2, c.NT, c.D, c.SEQ
    nps = NT // SEQ
    nprompt = c.NPROMPT_CORES * nps
    y_sample = np.stack([np.ascontiguousarray(res[ci]["yT"].T) for ci in range(c.NSAMP)], 0).astype(np.float32)
    y_prompt = np.zeros((nprompt, SEQ, D), np.float32)
    nk = np.zeros((nprompt, NO, SEQ, c.NKV, c.HD), np.float32)
    nv = np.zeros((nprompt, NO, SEQ, c.NKV, c.HD), np.float32)
    ns = np.zeros((nprompt, NO, 2, c.RH, 64, 64), np.float32)
    cps = SEQ // 128
    for pc in range(c.NPROMPT_CORES):
        r = res[c.NSAMP + pc]
        y_prompt[pc * nps:(pc + 1) * nps] = r["yT"].T.reshape(nps, SEQ, D)
        kT = r["kT_out"].reshape(NO, c.NKV, c.HD, NT)
        vo = r["v_out"].reshape(NO, NT, c.NKV, c.HD)
        so = r["st_out"].reshape(NO, 2, c.NCH, 2, 64, c.HC, 64)
        for s_ in range(nps):
            b = pc * nps + s_
            for oi in range(NO):
                nk[b, oi] = kT[oi][:, :, s_ * SEQ:(s_ + 1) * SEQ].transpose(2, 0, 1)
                nv[b, oi] = vo[oi, s_ * SEQ:(s_ + 1) * SEQ]
                for d in range(2):
                    ch = (s_ + 1) * cps - 1 if d == 0 else s_ * cps
                    ns[b, oi, d] = so[oi, d, ch].transpose(2, 0, 3, 1).reshape(c.RH, 64, 64)
    return (y_prompt, y_sample, nk, nv, ns)


def kernel(**inputs):
    return run_cfg(FULL, inputs)
```

```python
import math
import numpy as np
import ml_dtypes
import concourse.bass as bass
import concourse.mybir as mybir
from concourse.bass_utils import run_bass_kernel_spmd

F32 = mybir.dt.float32
BF16 = mybir.dt.bfloat16
AF = mybir.ActivationFunctionType
ALU = mybir.AluOpType
NPBF = ml_dtypes.bfloat16


class Cfg:
    def __init__(self, **kw):
        self.D = 4096
        self.L = 2
        self.NT = 1024
        self.SEQ = 256
        self.DEC_SEQ = 1024
        self.PAST = 512
        self.GRID_W = 64
        self.DFF = 11008
        self.GATE_LORA = 480
        self.DECAY_LORA = 128
        self.ICLR_LORA = 128
        self.NKV = 4
        self.HD = 128
        self.NSAMP = 4
        self.NPROMPT_CORES = 4
        self.POOL_WINDOWS = (2, 4, 8, 16)
        self.stop_after = None
        for k, v in kw.items():
            setattr(self, k, v)
        D = self.D
        self.TH = min(512, self.NT // 2)
        self.NH = self.NT // self.TH
        self.TT = self.NT // 128
        self.DC = D // 128
        self.FC = self.DFF // 128
        self.MIXW = D
        self.POOL_W = D // 2
        self.GW = self.POOL_W // 4
        self.GC = self.GW // 128
        self.RW = D // 2
        self.RH = self.RW // 64
        self.HC = self.RW // 128
        self.ATT_W = D - self.RW
        self.NQ = self.ATT_W // self.HD
        self.GQA = self.NQ // self.NKV
        self.KVW = self.NKV * self.HD
        self.RCOLS = 3 * self.RW + self.GATE_LORA + 2 * self.DECAY_LORA + self.ICLR_LORA
        self.ODD_COLS = self.RCOLS + self.ATT_W + 2 * self.KVW
        self.GLC = (self.GATE_LORA + 127) // 128
        self.NKEY = self.PAST + self.NT
        self.KT = self.NKEY // 128
        self.NCH = self.NT // 128
        self.ALPHA = (2 * self.L) ** 0.25
        self.NSUB = 3


FULL = Cfg()


class Buf:
    def __init__(self, name, t=None):
        self.name = name
        self.t = t
        self.w = {}
        self.r = {}
        self.dma_sem = None
        self.dma_cnt = 0
        self.excl = False

    def __getitem__(self, idx):
        return self.t[idx]


class EngState:
    def __init__(self, name, h, sem):
        self.name = name
        self.h = h
        self.sem = sem
        self.count = 0
        self.waited = {}


class Sched:
    def __init__(self, nc):
        self.nc = nc
        self.sems = {}
        self.E = {}
        for name, h in (("pe", nc.tensor), ("act", nc.scalar), ("dve", nc.vector),
                        ("pool", nc.gpsimd), ("sp", nc.sync)):
            sem = nc.alloc_semaphore(name="sem_" + name)
            self.sems[id(sem)] = sem
            self.E[name] = EngState(name, h, sem)
        self.n_instr = 0
        self.dma_bufs = []
        self.uid = 0

    def sbuf(self, name, shape, dtype):
        self.uid += 1
        return Buf(name, self.nc.alloc_sbuf_tensor(f"{name}_{self.uid}", list(shape), dtype))

    def sbuf_at(self, name, shape, dtype, offset):
        self.uid += 1
        assert offset % 32 == 0, (name, offset)
        return Buf(name, self.nc.alloc_sbuf_tensor_at(f"{name}_{self.uid}", list(shape), dtype, offset=offset))

    def psum(self, name, shape, dtype=F32):
        self.uid += 1
        b = Buf(name, self.nc.alloc_psum_tensor(f"{name}_{self.uid}", list(shape), dtype))
        b.excl = True
        return b

    def region(self, name):
        return Buf(name)

    def _dma_sem(self, b):
        if b.dma_sem is None:
            self.uid += 1
            b.dma_sem = self.nc.alloc_semaphore(name=f"ds_{b.name}_{self.uid}")
            self.sems[id(b.dma_sem)] = b.dma_sem
            self.dma_bufs.append(b)
        return b.dma_sem

    def _deps(self, reads, writes, merge):
        deps = {}

        def upd(d):
            for k, v in d.items():
                if deps.get(k, 0) < v:
                    deps[k] = v
        for b in reads:
            upd(b.w)
            if b.excl:
                upd(b.r)
        for b in writes:
            if not merge:
                upd(b.w)
            upd(b.r)
        return deps

    def _wait(self, eng, deps, skip_own=False):
        for k, v in deps.items():
            if skip_own and k == id(eng.sem):
                continue
            if eng.waited.get(k, 0) < v:
                eng.h.wait_ge(self.sems[k], v)
                eng.waited[k] = v

    def _record(self, ev, reads, writes, merge):
        k, v = ev
        for b in reads:
            if b.r.get(k, 0) < v:
                b.r[k] = v
        for b in writes:
            if merge:
                if b.w.get(k, 0) < v:
                    b.w[k] = v
            else:
                b.w = {k: v}
                b.r = {}

    def op(self, eng_name, fn, reads=(), writes=(), merge=False):
        eng = self.E[eng_name]
        self._wait(eng, self._deps(reads, writes, merge), skip_own=(eng_name == "pe"))
        ins = fn(eng.h)
        eng.count += 1
        ins.then_inc(eng.sem, 1)
        self._record((id(eng.sem), eng.count), reads, writes, merge)
        self.n_instr += 1
        return ins

    def mm(self, out_buf, out_ap, pairs, reads, merge=False, start=True, stop=True, transpose=False):
        eng = self.E["pe"]
        self._wait(eng, self._deps(reads, [out_buf], merge), skip_own=True)
        n = len(pairs)
        ins = None
        for i, (l, r) in enumerate(pairs):
            if transpose:
                ins = eng.h.transpose(out_ap, l, r)
            else:
                ins = eng.h.matmul(out_ap, l, r, start=(start and i == 0), stop=(stop and i == n - 1))
            self.n_instr += 1
        eng.count += 1
        ins.then_inc(eng.sem, 1)
        self._record((id(eng.sem), eng.count), reads, [out_buf], merge)
        return ins

    def dma(self, q, out_ap, in_ap, sem_buf, reads=(), writes=(), merge=False):
        eng = self.E[q]
        self._wait(eng, self._deps(reads, writes, merge))
        sem = self._dma_sem(sem_buf)
        ins = eng.h.dma_start(out=out_ap, in_=in_ap)
        ins.then_inc(sem, 16)
        sem_buf.dma_cnt += 16
        self._record((id(sem), sem_buf.dma_cnt), reads, writes, merge)
        self.n_instr += 1
        return ins

    def barrier(self):
        deps = {}
        for e in self.E.values():
            if e.count:
                deps[id(e.sem)] = e.count
        for b in self.dma_bufs:
            if b.dma_cnt:
                deps[id(b.dma_sem)] = b.dma_cnt
        for e in self.E.values():
            self._wait(e, deps, skip_own=True)

    def finish(self):
        deps = {}
        for e in self.E.values():
            if e.count:
                deps[id(e.sem)] = e.count
        for b in self.dma_bufs:
            if b.dma_cnt:
                deps[id(b.dma_sem)] = b.dma_cnt
        self._wait(self.E["sp"], deps, skip_own=True)


class Rot:
    def __init__(self, bufs):
        self.bufs = bufs
        self.i = 0

    def next(self):
        b = self.bufs[self.i % len(self.bufs)]
        self.i += 1
        return b


class Program:
    def __init__(self, cfg):
        self.cfg = cfg
        c = cfg
        self.nc = nc = bass.Bass("TRN2", target_bir_lowering=False)
        self.S = S = Sched(nc)
        self.din = {}
        self.dout = {}
        self.regions = {}
        D, NT, DC = c.D, c.NT, c.DC

        def inp(name, shape):
            self.din[name] = nc.dram_tensor(name, list(shape), F32, kind="ExternalInput").ap()
            return self.din[name]

        def outp(name, shape):
            self.dout[name] = nc.dram_tensor(name, list(shape), F32, kind="ExternalOutput").ap()
            return self.dout[name]

        def scratch(name, shape, dtype=F32):
            return nc.dram_tensor(name, list(shape), dtype, kind="Internal").ap()

        self.inp, self.outp, self.scratch = inp, outp, scratch
        inp("xT", [D, NT])
        inp("cond", [128, DC])
        inp("w_ada", [c.L * D, 9 * D])
        inp("b_adaT", [128, c.L * 9 * DC])
        inp("lnT", [128, c.L * 3 * 2 * DC])
        inp("ffn_in", [c.L * 2 * c.FC * 128, DC * 256])
        inp("ffn_out", [c.L * 2 * DC * 128, c.FC * 128])
        outp("yT", [D, NT])
        self.X = [self.din["xT"]] + [scratch(f"X{i}", [D, NT]) for i in range(1, c.L * 3)] + [self.dout["yT"]]
        self.Z = scratch("Z", [D, NT])

        self.modT = S.sbuf("modT", [128, c.L * 9 * DC], F32)
        self.sc1T = S.sbuf("sc1T", [128, c.L * 9 * DC], F32)
        self.lnT = S.sbuf("lnT", [128, c.L * 3 * 2 * DC], F32)
        self.identF = S.sbuf("identF", [128, 128], F32)
        self.onesF = S.sbuf("onesF", [128, 128], F32)
        self.onesB = S.sbuf("onesB", [128, 128], BF16)
        TH = c.TH
        self.xin = Rot([S.sbuf(f"xin{i}", [128, TH], F32) for i in range(2)])
        self.oslot = Rot([S.sbuf(f"osl{i}", [128, TH], F32) for i in range(2)])
        self.tmpA = Rot([S.sbuf(f"tmpA{i}", [128, TH], F32) for i in range(2)])
        self.tmpB = Rot([S.sbuf(f"tmpB{i}", [128, TH], F32) for i in range(2)])
        self.zsl = Rot([S.sbuf(f"zsl{i}", [128, TH], F32) for i in range(2)])
        self.mean = S.sbuf("mean", [128, TH], F32)
        self.rstd = S.sbuf("rstd", [128, TH], F32)
        self.nmr = S.sbuf("nmr", [128, TH], F32)
        self.var = S.sbuf("var", [128, TH], F32)
        self.bank = [S.psum(f"bank{i}", [128, 512], F32) for i in range(8)]
        reserve = 11 * 1024
        a0 = (nc.sbuf_base + 63) // 64 * 64
        asz = (nc.sbuf_top - a0 - reserve) // 64 * 64
        self._slab = nc.alloc_sbuf_tensor("arena_slab", [128, (a0 - nc.sbuf_base + asz) // 4], F32)
        self.arena0 = a0
        self.arena_end = a0 + asz
        nb = TH * 4
        ex = [self.at(f"xtra{i}", [128, TH], F32, (asz - (i + 1) * nb) // 64 * 64) for i in range(3)]
        self.xin = Rot(self.xin.bufs + [ex[0]])
        self.zsl = Rot(self.zsl.bufs + [ex[1], ex[2]])

    def reg(self, name):
        if name not in self.regions:
            self.regions[name] = self.S.region(name)
        return self.regions[name]

    def at(self, name, shape, dtype, off):
        nbytes = int(np.prod(shape[1:])) * (2 if dtype == BF16 else 4)
        assert self.arena0 + off + nbytes <= self.arena_end, (name, off, nbytes, self.arena_end - self.arena0)
        return self.S.sbuf_at(name, shape, dtype, self.arena0 + off)

    def mi(self, l, s, j, ch):
        return ((l * 3 + s) * 3 + j) * self.cfg.DC + ch

    def lni(self, l, s, gb, ch):
        return ((l * 3 + s) * 2 + gb) * self.cfg.DC + ch

    def phase_init(self):
        c, S, nc = self.cfg, self.S, self.nc
        DC = c.DC
        S.dma("sp", self.lnT[:], self.din["lnT"][:, :], self.lnT, writes=[self.lnT])
        S.op("dve", lambda h: h.memset(self.onesF[:], 1.0), writes=[self.onesF])
        S.op("dve", lambda h: h.memset(self.onesB[:], 1.0), writes=[self.onesB])
        idsrc = self.inp("ident", [128, 128])
        S.dma("sp", self.identF[:], idsrc[:, :], self.identF, writes=[self.identF])

    def phase_ada(self):
        c, S, nc = self.cfg, self.S, self.nc
        D, DC = c.D, c.DC
        cond = S.sbuf("cond", [128, DC], F32)
        scT = S.sbuf("scT", [128, DC], BF16)
        badaT = self.at("badaT", [128, c.L * 9 * DC], F32, 0)
        off = c.L * 9 * DC * 4
        off = (off + 63) // 64 * 64
        wsl = Rot([self.at(f"adaw{i}", [128, DC, 512], BF16, off + i * DC * 1024) for i in range(2)])
        rowb = Rot([S.sbuf(f"rowb{i}", [1, 512], F32) for i in range(1)])
        S.dma("sp", cond[:], self.din["cond"][:, :], cond, writes=[cond])
        S.dma("sp", badaT[:], self.din["b_adaT"][:, :], badaT, writes=[badaT])
        S.op("act", lambda h: h.activation(out=scT[:], in_=cond[:], func=AF.Silu), reads=[cond], writes=[scT])
        nblk = 9 * D // 512
        prow = self.bank[4]
        pmod = self.bank[5]
        for l in range(c.L):
            for cb in range(nblk):
                w = wsl.next()
                src = self.din["w_ada"][l * D:(l + 1) * D, cb * 512:(cb + 1) * 512].rearrange("(kc p) n -> p kc n", p=128)
                S.dma("pool", w[:], src, w, writes=[w])
                S.mm(prow, prow[0:1, 0:512], [(scT[:, kc:kc + 1], w[:, kc, :]) for kc in range(DC)], reads=[scT, w])
                rb = rowb.next()
                S.op("act", lambda h: h.activation(out=rb[:], in_=prow[0:1, 0:512], func=AF.Copy), reads=[prow], writes=[rb])
                for j in range(4):
                    S.mm(pmod, pmod[:, j:j + 1], [(rb[0:1, j * 128:(j + 1) * 128], self.onesF[0:1, 0:1])], reads=[rb, self.onesF], merge=(j > 0))
                base = l * 9 * DC + cb * 4
                S.op("dve", lambda h: h.tensor_tensor(out=self.modT[:, base:base + 4], in0=pmod[:, 0:4], in1=badaT[:, base:base + 4], op=ALU.add),
                     reads=[pmod, badaT], writes=[self.modT], merge=True)
        for l in range(c.L):
            for s in range(3):
                a = self.mi(l, s, 1, 0)
                S.op("dve", lambda h: h.tensor_scalar(out=self.sc1T[:, a:a + DC], in0=self.modT[:, a:a + DC], scalar1=1.0, scalar2=None, op0=ALU.add),
                     reads=[self.modT], writes=[self.sc1T], merge=True)
                g = self.mi(l, s, 2, 0)
                f = 1.0 if s == 1 else 0.5
                S.op("dve", lambda h: h.tensor_scalar(out=self.sc1T[:, g:g + DC], in0=self.modT[:, g:g + DC], scalar1=f, scalar2=None, op0=ALU.mult),
                     reads=[self.modT], writes=[self.sc1T], merge=True)
        S.barrier()

    def make_uT(self, l, s, Xin, uT, t0, ntok):
        c, S = self.cfg, self.S
        for ch in range(c.DC):
            for q0 in range(0, ntok, c.TH):
                xs = self.xin.next()
                S.dma("sp", xs[:], Xin[ch * 128:(ch + 1) * 128, t0 + q0:t0 + q0 + c.TH], xs,
                      reads=[self.reg(f"X{id(Xin)}")], writes=[xs])
                i_sc, i_sh = self.mi(l, s, 1, ch), self.mi(l, s, 0, ch)
                S.op("act", lambda h: h.activation(out=uT[:, ch, q0:q0 + c.TH], in_=xs[:], func=AF.Identity,
                                                   scale=self.sc1T[:, i_sc:i_sc + 1], bias=self.modT[:, i_sh:i_sh + 1]),
                     reads=[xs, self.sc1T, self.modT], writes=[uT], merge=True)

    def outproj_ln(self, l, s, hf, rhs_fn, rhs_bufs, nk, wsrc_row0, wsrc, wslots, Xin, Xout):
        c, S = self.cfg, self.S
        DC, TH, D = c.DC, c.TH, c.D
        t0 = hf * TH
        psY = [self.bank[4], self.bank[5]]
        psS1, psS2 = self.bank[6], self.bank[7]
        Xr_in, Xr_out, Zr = self.reg(f"X{id(Xin)}"), self.reg(f"X{id(Xout)}"), self.reg("Z")
        pend = None
        for dc in range(DC):
            w = wslots.next()
            S.dma("pool", w[:], wsrc[wsrc_row0 + dc * 128: wsrc_row0 + (dc + 1) * 128, :], w, writes=[w])
            py = psY[dc % 2]
            S.mm(py, py[:, 0:TH], [(w[:, k * 128:(k + 1) * 128], rhs_fn(k)) for k in range(nk)], reads=[w] + rhs_bufs)
            if pend is not None:
                self._stats_mm(*pend)
            xs = self.xin.next()
            S.dma("sp", xs[:], Xin[dc * 128:(dc + 1) * 128, t0:t0 + TH], xs, reads=[Xr_in], writes=[xs])
            t1 = self.tmpA.next()
            ig = self.mi(l, s, 2, dc)
            S.op("act", lambda h: h.activation(out=t1[:], in_=py[:, 0:TH], func=AF.Copy, scale=self.sc1T[:, ig:ig + 1]),
                 reads=[py, self.sc1T], writes=[t1])
            z = self.zsl.next()
            S.op("dve", lambda h: h.scalar_tensor_tensor(out=z[:], in0=xs[:], scalar=float(c.ALPHA), in1=t1[:], op0=ALU.mult, op1=ALU.add),
                 reads=[xs, t1], writes=[z])
            zq = self.tmpB.next()
            S.op("act", lambda h: h.activation(out=zq[:], in_=z[:], func=AF.Square), reads=[z], writes=[zq])
            S.dma("sp", self.Z[dc * 128:(dc + 1) * 128, t0:t0 + TH], z[:], z, reads=[z], writes=[Zr], merge=True)
            pend = (dc, z, zq, psS1, psS2)
        self._stats_mm(*pend)
        invD = 1.0 / D
        S.op("act", lambda h: h.activation(out=self.mean[:], in_=psS1[:, 0:TH], func=AF.Copy, scale=invD), reads=[psS1], writes=[self.mean])
        msq = self.tmpA.next()
        S.op("dve", lambda h: h.tensor_tensor(out=msq[:], in0=self.mean[:], in1=self.mean[:], op=ALU.mult), reads=[self.mean], writes=[msq])
        S.op("dve", lambda h: h.scalar_tensor_tensor(out=self.var[:], in0=psS2[:, 0:TH], scalar=invD, in1=msq[:], op0=ALU.mult, op1=ALU.subtract),
             reads=[psS2, msq], writes=[self.var])
        S.op("dve", lambda h: h.tensor_scalar(out=self.var[:], in0=self.var[:], scalar1=1e-5, scalar2=None, op0=ALU.add), reads=[self.var], writes=[self.var])
        S.op("act", lambda h: h.activation(out=self.var[:], in_=self.var[:], func=AF.Sqrt), reads=[self.var], writes=[self.var])
        S.op("dve", lambda h: h.reciprocal(out=self.rstd[:], in_=self.var[:]), reads=[self.var], writes=[self.rstd])
        S.op("dve", lambda h: h.scalar_tensor_tensor(out=self.nmr[:], in0=self.mean[:], scalar=-1.0, in1=self.rstd[:], op0=ALU.mult, op1=ALU.mult),
             reads=[self.mean, self.rstd], writes=[self.nmr])
        for dc in range(DC):
            z = self.zsl.next()
            S.dma("sp", z[:], self.Z[dc * 128:(dc + 1) * 128, t0:t0 + TH], z, reads=[Zr], writes=[z])
            t1 = self.tmpA.next()
            S.op("dve", lambda h: h.tensor_tensor(out=t1[:], in0=z[:], in1=self.rstd[:], op=ALU.mult), reads=[z, self.rstd], writes=[t1])
            t2 = self.tmpB.next()
            S.op("dve", lambda h: h.tensor_tensor(out=t2[:], in0=t1[:], in1=self.nmr[:], op=ALU.add), reads=[t1, self.nmr], writes=[t2])
            o = self.oslot.next()
            ig, ib = self.lni(l, s, 0, dc), self.lni(l, s, 1, dc)
            S.op("act", lambda h: h.activation(out=o[:], in_=t2[:], func=AF.Identity, scale=self.lnT[:, ig:ig + 1], bias=self.lnT[:, ib:ib + 1]),
                 reads=[t2, self.lnT], writes=[o])
            S.dma("pool", Xout[dc * 128:(dc + 1) * 128, t0:t0 + TH], o[:], o, reads=[o], writes=[Xr_out], merge=True)

    def _stats_mm(self, dc, z, zq, psS1, psS2):
        c, S = self.cfg, self.S
        TH, DC = c.TH, c.DC
        S.mm(psS1, psS1[:, 0:TH], [(self.onesF[:], z[:])], reads=[self.onesF, z], merge=(dc > 0), start=(dc == 0), stop=(dc == DC - 1))
        S.mm(psS2, psS2[:, 0:TH], [(self.onesF[:], zq[:])], reads=[self.onesF, zq], merge=(dc > 0), start=(dc == 0), stop=(dc == DC - 1))

    def phase_ffn(self, l, s, Xin, Xout):
        c, S = self.cfg, self.S
        DC, FC, TH = c.DC, c.FC, c.TH
        fi = 0 if s == 0 else 1
        uT = self.at("uT", [128, DC, TH], BF16, 0)
        hT = self.at("hT", [128, FC, TH], BF16, DC * TH * 2)
        woff = DC * TH * 2 + FC * TH * 2
        wgu = Rot([self.at(f"wgu{i}", [128, DC * 256], BF16, woff + i * DC * 512) for i in range(2)])
        wo = Rot([self.at("wo0", [128, FC * 128], BF16, 0), self.at("wo1", [128, FC * 128], BF16, woff)])
        sg = Rot([S.sbuf(f"sg{i}_{l}{s}", [128, TH], F32) for i in range(2)]) if not hasattr(self, "_sg") else self._sg
        self._sg = sg
        in_row0 = (l * 2 + fi) * FC * 128
        out_row0 = (l * 2 + fi) * DC * 128
        for hf in range(c.NH):
            self.make_uT(l, s, Xin, uT, hf * TH, TH)
            for f in range(FC):
                w = wgu.next()
                S.dma("pool", w[:], self.din["ffn_in"][in_row0 + f * 128: in_row0 + (f + 1) * 128, :], w, writes=[w])
                pg, pu = self.bank[(f % 2) * 2], self.bank[(f % 2) * 2 + 1]
                S.mm(pg, pg[:, 0:TH], [(w[:, kc * 256:kc * 256 + 128], uT[:, kc, :]) for kc in range(DC)], reads=[w, uT])
                S.mm(pu, pu[:, 0:TH], [(w[:, kc * 256 + 128:kc * 256 + 256], uT[:, kc, :]) for kc in range(DC)], reads=[w, uT])
                g = sg.next()
                S.op("act", lambda h: h.activation(out=g[:], in_=pg[:, 0:TH], func=AF.Silu), reads=[pg], writes=[g])
                S.op("dve", lambda h: h.tensor_tensor(out=hT[:, f, :], in0=pu[:, 0:TH], in1=g[:], op=ALU.mult), reads=[pu, g], writes=[hT], merge=True)
            S.barrier()
            self.outproj_ln(l, s, hf, lambda k: hT[:, k, :], [hT], FC, out_row0, self.din["ffn_out"], wo, Xin, Xout)
            S.barrier()


    def phase_even(self, l, Xin, Xout):
        c, S = self.cfg, self.S
        DC, TH, TT, GC, GW, NT = c.DC, c.TH, c.TT, c.GC, c.GW, c.NT
        MC = c.MIXW // 128
        PC = c.POOL_W // 128
        ei = l // 2
        if "ev_in" not in self.din:
            self.inp("ev_in", [((c.L + 1) // 2) * c.D, c.MIXW])
            self.inp("ev_out", [((c.L + 1) // 2) * DC * 128, MC * 128])
            self.inp("pool_w", [((c.L + 1) // 2) * 4 * GW, GW])
            self.inp("pool_scT", [128, ((c.L + 1) // 2) * PC])
            self.inp("poolM", [128, 4 * TT * 3 * 128])
            self.inp("cosN", [128, TT * NT])
            self.inp("sinN", [128, TT * NT])
            self.inp("cosC", [128, GC * GW])
            self.inp("nsinC", [128, GC * GW])
            self.pscT = S.sbuf("pscT", [128, ((c.L + 1) // 2) * PC], F32)
            S.dma("sp", self.pscT[:], self.din["pool_scT"][:, :], self.pscT, writes=[self.pscT])
        o_p = max(DC * TH * 2, 4 * GC * TH * 2 + 2 * GC * TH * 2 + 2 * GC * GW * 2, 2 * MC * 256)
        o_p = (o_p + 63) // 64 * 64
        o_mix = o_p + TT * c.MIXW * 2
        o_st = o_mix + MC * TH * 2
        uT = self.at("e_uT", [128, DC, TH], BF16, 0)
        pT = self.at("e_p", [128, TT, c.MIXW], BF16, o_p)
        mixT = self.at("e_mix", [128, MC, TH], BF16, o_mix)
        wblk = Rot([self.at(f"e_w{i}", [128, DC, 128], BF16, o_st + i * DC * 256) for i in range(2)])
        dT = self.at("e_dT", [128, 4 * GC, TH], BF16, 0)
        o1 = 4 * GC * TH * 2
        PcPs = self.at("e_pcps", [128, 2, GC, TH], BF16, o1)
        o2 = o1 + 2 * GC * TH * 2
        cosC = self.at("e_cosC", [128, GC, GW], BF16, o2)
        nsinC = self.at("e_nsinC", [128, GC, GW], BF16, o2 + GC * GW * 2)
        assert o2 + 2 * GC * GW * 2 <= o_p
        pm = self.at("e_pm", [128, TT, 3, 128], BF16, o_st)
        o3 = o_st + TT * 3 * 128 * 2
        pw = self.at("e_pw", [128, GC, GW], BF16, o3)
        o4 = o3 + GC * GW * 2
        cosN = self.at("e_cosN", [128, TT, TH], BF16, o4)
        sinN = self.at("e_sinN", [128, TT, TH], BF16, o4 + TT * TH * 2)
        wo = Rot([self.at(f"e_wo{i}", [128, MC * 128], BF16, i * MC * 256) for i in range(2)])
        assert 2 * MC * 256 <= o_p
        ev_in = self.din["ev_in"]
        tph = TH // 128
        for hf in range(c.NH):
            self.make_uT(l, 1, Xin, uT, hf * TH, TH)
            for cb in range(MC):
                w = wblk.next()
                src = ev_in[ei * c.D:(ei + 1) * c.D, cb * 128:(cb + 1) * 128].rearrange("(kc p) n -> p kc n", p=128)
                S.dma("pool", w[:], src, w, writes=[w])
                for tl in range(tph):
                    tt = hf * tph + tl
                    pb = self.bank[(cb * tph + tl) % 4]
                    S.mm(pb, pb[:, 0:128], [(uT[:, kc, tl * 128:(tl + 1) * 128], w[:, kc, :]) for kc in range(DC)], reads=[uT, w])
                    if (cb + tl) % 2 == 0:
                        S.op("act", lambda h: h.activation(out=pT[:, tt, cb * 128:(cb + 1) * 128], in_=pb[:, 0:128], func=AF.Copy), reads=[pb], writes=[pT], merge=True)
                    else:
                        S.op("dve", lambda h: h.tensor_copy(out=pT[:, tt, cb * 128:(cb + 1) * 128], in_=pb[:, 0:128]), reads=[pb], writes=[pT], merge=True)
        S.barrier()
        S.dma("pool", cosC[:], self.din["cosC"].rearrange("p (a b) -> p a b", a=GC), cosC, writes=[cosC])
        S.dma("pool", nsinC[:], self.din["nsinC"].rearrange("p (a b) -> p a b", a=GC), nsinC, writes=[nsinC])
        for hf in range(c.NH):
            t0 = hf * TH
            for g in range(4):
                S.dma("pool", pm[:], self.din["poolM"][:, g * TT * 384:(g + 1) * TT * 384].rearrange("p (a b c) -> p a b c", a=TT, b=3), pm, writes=[pm])
                S.dma("pool", pw[:], self.din["pool_w"][(ei * 4 + g) * GW:(ei * 4 + g + 1) * GW, :].rearrange("(a p) n -> p a n", p=128), pw, writes=[pw])
                for cc in range(GC):
                    col0 = g * GW + cc * 128
                    pb = self.bank[cc % 2]
                    for tl in range(tph):
                        j = hf * tph + tl
                        pairs = []
                        for nb in range(3):
                            i = j + nb - 1
                            if 0 <= i < TT:
                                pairs.append((pT[:, i, col0:col0 + 128], pm[:, j, nb, :]))
                        S.mm(pb, pb[:, tl * 128:(tl + 1) * 128], pairs, reads=[pT, pm], merge=(tl > 0))
                    S.op("act", lambda h: h.activation(out=dT[:, g * GC + cc, :], in_=pb[:, 0:TH], func=AF.Copy), reads=[pb], writes=[dT], merge=True)
                for d2 in range(GC):
                    pb = self.bank[2 + d2 % 2]
                    S.mm(pb, pb[:, 0:TH], [(pw[:, cc, d2 * 128:(d2 + 1) * 128], dT[:, g * GC + cc, :]) for cc in range(GC)], reads=[pw, dT])
                    isc = ei * PC + g * GC + d2
                    S.op("act", lambda h: h.activation(out=mixT[:, g * GC + d2, :], in_=pb[:, 0:TH], func=AF.Copy, scale=self.pscT[:, isc:isc + 1]),
                         reads=[pb, self.pscT], writes=[mixT], merge=True)
            S.dma("pool", cosN[:], self.din["cosN"].rearrange("p (a b) -> p a b", a=TT)[:, :, t0:t0 + TH], cosN, writes=[cosN])
            S.dma("pool", sinN[:], self.din["sinN"].rearrange("p (a b) -> p a b", a=TT)[:, :, t0:t0 + TH], sinN, writes=[sinN])
            for g in range(4):
                for cc in range(GC):
                    col0 = c.POOL_W + g * GW + cc * 128
                    for si, tab in enumerate((cosN, sinN)):
                        pb = self.bank[si]
                        S.mm(pb, pb[:, 0:TH], [(pT[:, i, col0:col0 + 128], tab[:, i, :]) for i in range(TT)], reads=[pT, tab])
                        if si == 0:
                            S.op("act", lambda h: h.activation(out=PcPs[:, si, cc, :], in_=pb[:, 0:TH], func=AF.Copy), reads=[pb], writes=[PcPs], merge=True)
                        else:
                            S.op("dve", lambda h: h.tensor_copy(out=PcPs[:, si, cc, :], in_=pb[:, 0:TH]), reads=[pb], writes=[PcPs], merge=True)
                for k2 in range(GC):
                    pb = self.bank[2 + k2 % 2]
                    pairs = [(cosC[:, cc, k2 * 128:(k2 + 1) * 128], PcPs[:, 0, cc, :]) for cc in range(GC)]
                    pairs += [(nsinC[:, cc, k2 * 128:(k2 + 1) * 128], PcPs[:, 1, cc, :]) for cc in range(GC)]
                    S.mm(pb, pb[:, 0:TH], pairs, reads=[cosC, nsinC, PcPs])
                    S.op("dve", lambda h: h.tensor_copy(out=mixT[:, PC + g * GC + k2, :], in_=pb[:, 0:TH]), reads=[pb], writes=[mixT], merge=True)
            S.barrier()
            self.outproj_ln(l, 1, hf, lambda k: mixT[:, k, :], [mixT], MC, ei * DC * 128, self.din["ev_out"], wo, Xin, Xout)
            S.barrier()
            if hf + 1 < c.NH:
                S.dma("pool", cosC[:], self.din["cosC"].rearrange("p (a b) -> p a b", a=GC), cosC, writes=[cosC])
                S.dma("pool", nsinC[:], self.din["nsinC"].rearrange("p (a b) -> p a b", a=GC), nsinC, writes=[nsinC])


    def odd_setup(self):
        c, S = self.cfg, self.S
        if hasattr(self, "_odd_ready"):
            return
        self._odd_ready = True
        NO = c.L // 2
        DC, NT, HC, MC = c.DC, c.NT, c.HC, c.MIXW // 128
        self.NFM = 3 * HC + c.GLC + 3 + c.NQ + c.NKV
        self.NRW = 3 * HC + c.GLC + 3
        for name, shape in (("od_in", [NO * self.NFM * 128, DC * 128]), ("od_inv", [NO * c.D, c.KVW]),
                            ("od_out", [NO * DC * 128, MC * 128]), ("muT", [128, NO * self.NRW]),
                            ("w0T", [128, NO * 2 * HC]), ("dw2", [NO * 2 * 128, c.RW]), ("a0T", [128, NO * HC]),
                            ("iw2", [NO * 128, c.RW]), ("gw2", [NO * c.GLC * 128, c.RW]), ("kkT", [128, NO * HC]),
                            ("kaT", [128, NO * HC]), ("rkT", [128, NO * HC]), ("gngT", [128, NO * HC]), ("gnbT", [128, NO * HC]),
                            ("qnT", [128, NO]), ("knT", [128, NO]),
                            ("mprev", [128, NT]), ("mnext", [128, NT]), ("ropeC", [128, NT]), ("ropeS", [128, NT]),
                            ("rotT", [128, 128]), ("Am", [8, NT]), ("Bm", [8, c.NKEY]),
                            ("cKT", [NO * c.NKV * 128, c.PAST]), ("cV", [NO * c.PAST, c.KVW]),
                            ("st0", [NO * 2 * 128, HC * 64]), ("mres", [128, 2 * c.NCH]),
                            ("masks", [128, 4 * 128]), ("bdones", [128, 128])):
            self.inp(name, shape)
        self.outp("kT_out", [NO * c.NKV * 128, NT])
        self.outp("v_out", [NO * NT, c.KVW])
        self.outp("st_out", [NO * 2 * c.NCH * 128, HC * 64])
        self.MIX = self.scratch("MIX", [MC * 128, NT])
        self.PR = self.scratch("PR", [3 * HC * 128, NT])
        nsm = NO * (self.NRW + 2 * HC + 6 * HC + 2)
        self.osm = S.sbuf("osm", [128, nsm + 2 * self.NRW * NO + 2 * c.NCH], F32)
        o = 0
        self.sm = {}
        for name, n in (("muT", NO * self.NRW), ("w0T", NO * 2 * HC), ("a0T", NO * HC), ("kkT", NO * HC), ("kaT", NO * HC),
                        ("rkT", NO * HC), ("gngT", NO * HC), ("gnbT", NO * HC), ("qnT", NO), ("knT", NO), ("mres", 2 * c.NCH)):
            self.sm[name] = o
            S.dma("sp", self.osm[:, o:o + n], self.din[name][:, :], self.osm, writes=[self.osm], merge=True)
            o += n
        self.sm["omm"] = o
        n = NO * self.NRW
        a = self.sm["muT"]
        S.op("dve", lambda h: h.tensor_scalar(out=self.osm[:, o:o + n], in0=self.osm[:, a:a + n], scalar1=-1.0, scalar2=1.0, op0=ALU.mult, op1=ALU.add),
             reads=[self.osm], writes=[self.osm])
        o2 = o + n
        self.sm["hmu"] = o2
        S.op("dve", lambda h: h.tensor_scalar(out=self.osm[:, o2:o2 + n], in0=self.osm[:, a:a + n], scalar1=0.5, scalar2=None, op0=ALU.mult),
             reads=[self.osm], writes=[self.osm])
        self.masks = S.sbuf("masks", [128, 4, 128], F32)
        S.dma("sp", self.masks[:], self.din["masks"].rearrange("p (a b) -> p a b", a=4), self.masks, writes=[self.masks])
        self.bdones = S.sbuf("bdones", [128, 128], F32)
        S.dma("sp", self.bdones[:], self.din["bdones"][:, :], self.bdones, writes=[self.bdones])
        self.rotT = S.sbuf("rotT", [128, 128], F32)
        S.dma("sp", self.rotT[:], self.din["rotT"][:, :], self.rotT, writes=[self.rotT])

    def smc(self, name, idx):
        o = self.sm[name] + idx
        return self.osm[:, o:o + 1]

    def phase_odd(self, l, Xin, Xout):
        c, S = self.cfg, self.S
        self.odd_setup()
        DC, NT, TH, HC, NH, MC = c.DC, c.NT, c.TH, c.HC, c.NH, c.MIXW // 128
        oi = l // 2
        NFM, NRW = self.NFM, self.NRW
        bump = [0]

        def A(name, shape, dtype):
            nb = int(np.prod(shape[1:])) * (2 if dtype == BF16 else 4)
            b = self.at("o_" + name, shape, dtype, bump[0])
            bump[0] = (bump[0] + nb + 63) // 64 * 64
            return b

        skip = getattr(c, "skip", ())
        if "pA" in skip:
            return
        uT = A("uT", [128, DC, NT], BF16)
        wfm = Rot([A(f"wfm{i}", [128, DC, 128], BF16) for i in range(2)])
        mprev = A("mprev", [128, NT], BF16)
        mnext = A("mnext", [128, NT], BF16)
        pf = A("pf", [128, NT + 2], F32)
        sgT = A("sgT", [128, c.GLC, NT], BF16)
        twT = A("twT", [128, 2, NT], BF16)
        alT = A("alT", [128, NT], BF16)
        base_after_shared = bump[0]
        T4 = Rot([A(f"T{i}", [128, NT], F32) for i in range(5)])
        S.dma("pool", mprev[:], self.din["mprev"][:, :], mprev, writes=[mprev])
        S.dma("pool", mnext[:], self.din["mnext"][:, :], mnext, writes=[mnext])
        S.op("dve", lambda h: h.memset(pf[:], 0.0), writes=[pf])
        self.make_uT(l, 1, Xin, uT, 0, NT)
        od_in = self.din["od_in"]

        def proj_fm(chunk, banks):
            w = wfm.next()
            r0 = (oi * NFM + chunk) * 128
            S.dma("pool", w[:], od_in[r0:r0 + 128, :].rearrange("p (a b) -> p a b", a=DC), w, writes=[w])
            for hf in range(NH):
                pb = banks[hf]
                S.mm(pb, pb[:, 0:TH], [(w[:, kc, :], uT[:, kc, hf * TH:(hf + 1) * TH]) for kc in range(DC)], reads=[w, uT])

        def shifted(chunk, banks, out_fn):
            for hf in range(NH):
                pb = banks[hf]
                S.op("act", lambda h: h.activation(out=pf[:, 1 + hf * TH:1 + (hf + 1) * TH], in_=pb[:, 0:TH], func=AF.Copy), reads=[pb], writes=[pf], merge=(hf > 0))
            t1, t2 = T4.next(), T4.next()
            S.op("dve", lambda h: h.tensor_tensor(out=t1[:], in0=pf[:, 0:NT], in1=mprev[:], op=ALU.mult), reads=[pf, mprev], writes=[t1])
            S.op("dve", lambda h: h.tensor_tensor(out=t2[:], in0=pf[:, 2:NT + 2], in1=mnext[:], op=ALU.mult), reads=[pf, mnext], writes=[t2])
            S.op("dve", lambda h: h.tensor_tensor(out=t1[:], in0=t1[:], in1=t2[:], op=ALU.add), reads=[t1, t2], writes=[t1])
            S.op("act", lambda h: h.activation(out=t2[:], in_=t1[:], func=AF.Copy, scale=self.smc("hmu", oi * NRW + chunk)), reads=[t1, self.osm], writes=[t2])
            S.op("dve", lambda h: h.scalar_tensor_tensor(out=t1[:], in0=pf[:, 1:NT + 1], scalar=self.smc("omm", oi * NRW + chunk), in1=t2[:], op0=ALU.mult, op1=ALU.add),
                 reads=[pf, t2, self.osm], writes=[t1])
            return t1

        pbA = [self.bank[4], self.bank[5]] if NH == 2 else [self.bank[4]]
        pbB = [self.bank[6], self.bank[7]] if NH == 2 else [self.bank[6]]
        if "pB" in skip:
            return
        cg0 = 3 * HC
        for gc in range(c.GLC):
            banks = pbA if gc % 2 == 0 else pbB
            proj_fm(cg0 + gc, banks)
            t = shifted(cg0 + gc, banks, None)
            S.op("act", lambda h: h.activation(out=sgT[:, gc, :], in_=t[:], func=AF.Sigmoid), reads=[t], writes=[sgT], merge=True)
        for d in range(2):
            banks = pbA if d % 2 == 0 else pbB
            proj_fm(cg0 + c.GLC + d, banks)
            t = shifted(cg0 + c.GLC + d, banks, None)
            S.op("act", lambda h: h.activation(out=twT[:, d, :], in_=t[:], func=AF.Tanh), reads=[t], writes=[twT], merge=True)
        proj_fm(cg0 + c.GLC + 2, pbA)
        t = shifted(cg0 + c.GLC + 2, pbA, None)
        S.op("act", lambda h: h.activation(out=alT[:], in_=t[:], func=AF.Copy), reads=[t], writes=[alT])
        if "pC" in skip:
            return
        PRr = self.reg("PR")
        for ch in range(3 * HC):
            banks = pbA if ch % 2 == 0 else pbB
            proj_fm(ch, banks)
            t = shifted(ch, banks, None)
            S.dma("sp", self.PR[ch * 128:(ch + 1) * 128, :], t[:], t, reads=[t], writes=[PRr], merge=True)
        if "pD" in skip:
            return
        if "att" not in getattr(c, "skip", ()):
            self.odd_attention(l, oi, uT, wfm, T4, proj_fm, A, pbA, pbB)
        S.barrier()
        if "rwkv" not in getattr(c, "skip", ()):
            self.odd_rwkv(l, oi, base_after_shared, sgT, twT, alT)
        S.barrier()
        if "pE" in skip:
            return
        mixT = self.at("o_mixT", [128, MC, TH], BF16, 0)
        wo = Rot([self.at(f"o_wo{i}", [128, MC * 128], BF16, MC * TH * 2 + i * MC * 256) for i in range(2)])
        MIXr = self.reg("MIX")
        for hf in range(NH):
            for k in range(MC):
                S.dma("pool", mixT[:, k, :], self.MIX[k * 128:(k + 1) * 128, hf * TH:(hf + 1) * TH], mixT, reads=[MIXr], writes=[mixT], merge=(k > 0))
            self.outproj_ln(l, 1, hf, lambda k: mixT[:, k, :], [mixT], MC, oi * DC * 128, self.din["od_out"], wo, Xin, Xout)
            S.barrier()

    def odd_attention(self, l, oi, uT, wfm, T4, proj_fm, A, pbA, pbB):
        c, S = self.cfg, self.S
        DC, NT, TH, HC, NH, KT = c.DC, c.NT, c.TH, c.HC, c.NH, c.KT
        PKT = c.PAST // 128
        NFM = self.NFM
        cq0 = 3 * HC + c.GLC + 3
        ck0 = cq0 + c.NQ
        ropeC = A("ropeC", [128, NT], F32)
        ropeS = A("ropeS", [128, NT], F32)
        Am = A("Am", [8, NT], BF16)
        Bm = A("Bm", [8, c.NKEY], BF16)
        KTg = A("KTg", [128, c.NKEY], BF16)
        Vg = A("Vg", [128, KT, 128], BF16)
        QT = A("QT", [128, NT], BF16)
        wv = A("wv", [128, DC, 128], BF16)
        vst = Rot([A(f"vst{i}", [128, 128], F32) for i in range(2)])
        Et = Rot([A(f"Et{i}", [128, TH], BF16) for i in range(3)])
        oT = Rot([A(f"oT{i}", [128, TH], F32) for i in range(2)])
        rden = A("rden", [128, TH], F32)
        S.dma("sp", ropeC[:], self.din["ropeC"][:, :], ropeC, writes=[ropeC])
        S.dma("sp", ropeS[:], self.din["ropeS"][:, :], ropeS, writes=[ropeS])
        S.dma("pool", Am[:], self.din["Am"][:, :], Am, writes=[Am])
        S.dma("pool", Bm[:], self.din["Bm"][:, :], Bm, writes=[Bm])
        MIXr = self.reg("MIX")
        scale = float(c.HD) ** -0.5

        def norm_rope(banks, gain_name, out_bf, out_col0, kout_rows=None):
            raw, sq, kn = T4.next(), T4.next(), T4.next()
            for hf in range(NH):
                sl = slice(hf * TH, (hf + 1) * TH)
                pb = banks[hf]
                S.op("act", lambda h: h.activation(out=raw[:, sl], in_=pb[:, 0:TH], func=AF.Copy), reads=[pb], writes=[raw], merge=(hf > 0))
                S.op("dve", lambda h: h.tensor_tensor(out=sq[:, sl], in0=pb[:, 0:TH], in1=raw[:, sl], op=ALU.mult), reads=[pb, raw], writes=[sq], merge=(hf > 0))
            for hf in range(NH):
                sl = slice(hf * TH, (hf + 1) * TH)
                pb = banks[hf]
                S.mm(pb, pb[:, 0:TH], [(self.onesF[:], sq[:, sl])], reads=[self.onesF, sq])
                S.op("act", lambda h: h.activation(out=sq[:, sl], in_=pb[:, 0:TH], func=AF.Sqrt, scale=1.0 / c.HD, bias=self.eps6[:, 0:1]), reads=[pb, self.eps6], writes=[sq])
            S.op("dve", lambda h: h.reciprocal(out=sq[:], in_=sq[:]), reads=[sq], writes=[sq])
            S.op("dve", lambda h: h.scalar_tensor_tensor(out=kn[:], in0=raw[:], scalar=self.smc(gain_name, oi), in1=sq[:], op0=ALU.mult, op1=ALU.mult),
                 reads=[raw, sq, self.osm], writes=[kn])
            if kout_rows is not None:
                S.dma("sp", self.dout["kT_out"][kout_rows:kout_rows + 128, :], kn[:], kn, reads=[kn])
            for hf in range(NH):
                sl = slice(hf * TH, (hf + 1) * TH)
                pb = banks[hf]
                S.mm(pb, pb[:, 0:TH], [(self.rotT[:], kn[:, sl])], reads=[self.rotT, kn])
                S.op("dve", lambda h: h.tensor_tensor(out=raw[:, sl], in0=pb[:, 0:TH], in1=ropeS[:, sl], op=ALU.mult), reads=[pb, ropeS], writes=[raw])
            S.op("dve", lambda h: h.tensor_tensor(out=sq[:], in0=kn[:], in1=ropeC[:], op=ALU.mult), reads=[kn, ropeC], writes=[sq])
            S.op("dve", lambda h: h.tensor_tensor(out=out_bf[:, out_col0:out_col0 + NT], in0=sq[:], in1=raw[:], op=ALU.add), reads=[sq, raw], writes=[out_bf])

        if not hasattr(self, "eps6"):
            self.eps6 = S.sbuf("eps6", [128, 2], F32)
            S.op("dve", lambda h: h.memset(self.eps6[:, 0:1], 1e-6), writes=[self.eps6])
            S.op("dve", lambda h: h.memset(self.eps6[:, 1:2], 1e-12), writes=[self.eps6])
        skip = getattr(c, "skip", ())
        if "a1" in skip:
            return
        for g in range(c.NKV):
            S.dma("pool", KTg[:, 0:c.PAST], self.din["cKT"][(oi * c.NKV + g) * 128:(oi * c.NKV + g + 1) * 128, :], KTg, writes=[KTg])
            proj_fm(ck0 + g, pbA)
            norm_rope(pbA, "knT", KTg, c.PAST, kout_rows=(oi * c.NKV + g) * 128)
            if "a2" in skip:
                continue
            if "a3b" not in skip:
              S.dma("pool", Vg[:, 0:PKT, :], self.din["cV"][oi * c.PAST:(oi + 1) * c.PAST, g * 128:(g + 1) * 128].rearrange("(a p) d -> p a d", p=128), Vg, writes=[Vg])
            S.dma("pool", wv[:], self.din["od_inv"][oi * c.D:(oi + 1) * c.D, g * 128:(g + 1) * 128].rearrange("(a p) d -> p a d", p=128), wv, writes=[wv])
            for tt in range(c.TT):
                if "a3c" in skip:
                    break
                pb = self.bank[tt % 2]
                S.mm(pb, pb[:, 0:128], [(uT[:, kc, tt * 128:(tt + 1) * 128], wv[:, kc, :]) for kc in range(DC)], reads=[uT, wv])
                vs = vst.next()
                if "a3d" not in skip:
                    S.op("act", lambda h: h.activation(out=vs[:], in_=pb[:, 0:128], func=AF.Copy), reads=[pb], writes=[vs])
                if "a3e" not in skip:
                    S.op("dve", lambda h: h.tensor_copy(out=Vg[:, PKT + tt, :], in_=pb[:, 0:128]), reads=[pb], writes=[Vg], merge=True)
                if "a3d" in skip:
                    continue
                if "a3a" not in skip:
                    S.dma("sp", self.dout["v_out"][oi * NT + tt * 128: oi * NT + (tt + 1) * 128, g * 128:(g + 1) * 128], vs[:], vs, reads=[vs])
            if "a3" in skip:
                continue
            for hq in range(c.GQA):
                h_ = g * c.GQA + hq
                proj_fm(cq0 + h_, pbB)
                norm_rope(pbB, "qnT", QT, 0)
                if "a4" in skip:
                    continue
                for hf in range(NH):
                    sl = slice(hf * TH, (hf + 1) * TH)
                    pO, pD = self.bank[2], self.bank[3]
                    pend = None
                    for kt in range(KT + 1):
                        if kt < KT:
                            ps_ = self.bank[kt % 2]
                            S.mm(ps_, ps_[:, 0:TH], [(KTg[:, kt * 128:(kt + 1) * 128], QT[:, sl]), (Bm[:, kt * 128:(kt + 1) * 128], Am[:, sl])], reads=[KTg, QT, Bm, Am])
                            e = Et.next()
                            S.op("act", lambda h: h.activation(out=e[:], in_=ps_[:, 0:TH], func=AF.Exp, scale=scale), reads=[ps_], writes=[e])
                        if pend is not None:
                            pk, pe_ = pend
                            S.mm(pO, pO[:, 0:TH], [(Vg[:, pk, :], pe_[:])], reads=[Vg, pe_], merge=(pk > 0), start=(pk == 0), stop=(pk == KT - 1))
                            S.mm(pD, pD[:, 0:TH], [(self.onesB[:], pe_[:])], reads=[self.onesB, pe_], merge=(pk > 0), start=(pk == 0), stop=(pk == KT - 1))
                        pend = (kt, e) if kt < KT else None
                    S.op("dve", lambda h: h.reciprocal(out=rden[:], in_=pD[:, 0:TH]), reads=[pD], writes=[rden])
                    o = oT.next()
                    S.op("dve", lambda h: h.tensor_tensor(out=o[:], in0=pO[:, 0:TH], in1=rden[:], op=ALU.mult), reads=[pO, rden], writes=[o])
                    r0 = (c.RW // 128 + h_) * 128
                    S.dma("sp", self.MIX[r0:r0 + 128, sl], o[:], o, reads=[o], writes=[MIXr], merge=True)


    def odd_rwkv(self, l, oi, base2, sgT, twT, alT):
        c, S = self.cfg, self.S
        DC, NT, TH, HC, NH, NCH, GLC = c.DC, c.NT, c.TH, c.HC, c.NH, c.NCH, c.GLC
        r1_end = DC * NT * 2
        st = {"o": 0, "second": False}

        def A(name, shape, dtype):
            nb = int(np.prod(shape[1:])) * (2 if dtype == BF16 else 4)
            nb = (nb + 63) // 64 * 64
            if not st["second"] and st["o"] + nb > r1_end:
                st["second"] = True
                st["o"] = max(base2, st["o"]) if st["o"] > r1_end else base2
            b = self.at("r_" + name, shape, dtype, st["o"])
            st["o"] += nb
            return b

        NEG = -math.exp(-0.5)
        v_ = A("v", [128, NT], F32)
        g_ = A("g", [128, NT], F32)
        yacc = A("yacc", [128, NT], F32)
        bonus = A("bonus", [128, NT], F32)
        At = [A(f"At{d}", [128, NT], F32) for d in range(2)]
        Rt = [A(f"Rt{d}", [128, NT], F32) for d in range(2)]
        Bt = [A(f"Bt{d}", [128, NT], F32) for d in range(2)]
        Kt = [A(f"Kt{d}", [128, NT], F32) for d in range(2)]
        pC = [A(f"pC{d}", [128, NCH], F32) for d in range(2)]
        ST = [A(f"ST{d}", [128, 64], F32) for d in range(2)]
        Send = [Rot([A(f"Send{d}{i}", [128, 64], F32) for i in range(2)]) for d in range(2)]
        dw2h = A("dw2h", [128, 2, 128], BF16)
        iw2h = A("iw2h", [128, 128], BF16)
        gw2h = A("gw2h", [128, GLC, 128], BF16)
        alias0 = dict(st)
        r_ = A("r", [128, NT], F32)
        k_ = A("k", [128, NT], F32)
        a_ = A("a", [128, NT], F32)
        kk_ = A("kk", [128, NT], F32)
        b_ = A("b", [128, NT], F32)
        lw = [A(f"lw{d}", [128, NT], F32) for d in range(2)]
        TP = Rot([A(f"tp{i}", [128, NT], F32) for i in range(5)])
        st.update(alias0)
        tm = [[A(f"tm{d}{i}", [128, 128], F32) for i in range(4)] for d in range(2)]
        IT = {}
        for d in range(2):
            for hp in range(2):
                IT[(d, hp)] = dict(
                    NT_=[A(f"PT{d}{hp}{i}", [128, 128], F32) for i in range(2)],
                    N=[A(f"P{d}{hp}{i}", [128, 128], F32) for i in range(2)],
                    MkT=A(f"MkT{d}{hp}", [128, 128], F32), QbT=A(f"QbT{d}{hp}", [128, 128], F32), QkT=A(f"QkT{d}{hp}", [128, 128], F32),
                    X=[A(f"X{d}{hp}{i}", [128, 128], F32) for i in range(2)])
        GT0s = [A(f"GT0s{d}", [128, 64], F32) for d in range(2)]
        H0p = [A(f"H0p{d}", [128, 64], F32) for d in range(2)]
        RhT = [A(f"RhT{d}", [128, 128], F32) for d in range(2)]
        PRr, MIXr = self.reg("PR"), self.reg("MIX")
        pbA = [self.bank[4], self.bank[5]] if NH == 2 else [self.bank[4]]
        pbB = [self.bank[6], self.bank[7]] if NH == 2 else [self.bank[6]]
        SU, SL, IU, IL = 0, 1, 2, 3
        mask_strict = [SU, SL]
        mask_incl = [IU, IL]

        def hsl(hf):
            return slice(hf * TH, (hf + 1) * TH)

        for hc in range(HC):
            S.dma("sp", r_[:], self.PR[hc * 128:(hc + 1) * 128, :], r_, reads=[PRr], writes=[r_])
            S.dma("sp", k_[:], self.PR[(HC + hc) * 128:(HC + hc + 1) * 128, :], k_, reads=[PRr], writes=[k_])
            S.dma("sp", v_[:], self.PR[(2 * HC + hc) * 128:(2 * HC + hc + 1) * 128, :], v_, reads=[PRr], writes=[v_])
            S.dma("pool", dw2h[:], self.din["dw2"][oi * 256:(oi + 1) * 256, hc * 128:(hc + 1) * 128].rearrange("(d p) n -> p d n", p=128), dw2h, writes=[dw2h])
            S.dma("pool", iw2h[:], self.din["iw2"][oi * 128:(oi + 1) * 128, hc * 128:(hc + 1) * 128], iw2h, writes=[iw2h])
            S.dma("pool", gw2h[:], self.din["gw2"][oi * GLC * 128:(oi + 1) * GLC * 128, hc * 128:(hc + 1) * 128].rearrange("(a p) n -> p a n", p=128), gw2h, writes=[gw2h])
            for d in range(2):
                S.dma("sp", ST[d][:], self.din["st0"][(oi * 2 + d) * 128:(oi * 2 + d + 1) * 128, hc * 64:(hc + 1) * 64], ST[d], writes=[ST[d]])
            S.op("dve", lambda h: h.memset(yacc[:], 0.0), writes=[yacc])
            for d in range(2):
                for hf in range(NH):
                    pb = pbA[hf]
                    S.mm(pb, pb[:, 0:TH], [(dw2h[:, d, :], twT[:, d, hsl(hf)])], reads=[dw2h, twT])
                    S.op("act", lambda h: h.activation(out=lw[d][:, hsl(hf)], in_=pb[:, 0:TH], func=AF.Sigmoid, bias=self.smc("w0T", (oi * 2 + d) * HC + hc)),
                         reads=[pb, self.osm], writes=[lw[d]], merge=(hf > 0))
                S.op("dve", lambda h: h.tensor_scalar(out=lw[d][:], in0=lw[d][:], scalar1=NEG, scalar2=None, op0=ALU.mult), reads=[lw[d]], writes=[lw[d]])
            for hf in range(NH):
                pb = pbB[hf]
                S.mm(pb, pb[:, 0:TH], [(iw2h[:], alT[:, hsl(hf)])], reads=[iw2h, alT])
                S.op("act", lambda h: h.activation(out=a_[:, hsl(hf)], in_=pb[:, 0:TH], func=AF.Sigmoid, bias=self.smc("a0T", oi * HC + hc)),
                     reads=[pb, self.osm], writes=[a_], merge=(hf > 0))
            for hf in range(NH):
                pb = pbA[hf]
                S.mm(pb, pb[:, 0:TH], [(gw2h[:, gc, :], sgT[:, gc, hsl(hf)]) for gc in range(GLC)], reads=[gw2h, sgT])
                S.op("act", lambda h: h.activation(out=g_[:, hsl(hf)], in_=pb[:, 0:TH], func=AF.Copy), reads=[pb], writes=[g_], merge=(hf > 0))
            t1, t2 = TP.next(), TP.next()
            S.op("act", lambda h: h.activation(out=t1[:], in_=k_[:], func=AF.Square, scale=self.smc("kkT", oi * HC + hc)), reads=[k_, self.osm], writes=[t1])
            for hf in range(NH):
                pb = pbB[hf]
                S.mm(pb, pb[:, 0:TH], [(self.bdones[:], t1[:, hsl(hf)])], reads=[self.bdones, t1])
                S.op("act", lambda h: h.activation(out=t2[:, hsl(hf)], in_=pb[:, 0:TH], func=AF.Sqrt, bias=self.eps6[:, 1:2]), reads=[pb, self.eps6], writes=[t2], merge=(hf > 0))
            S.op("dve", lambda h: h.reciprocal(out=t2[:], in_=t2[:]), reads=[t2], writes=[t2])
            S.op("dve", lambda h: h.scalar_tensor_tensor(out=kk_[:], in0=k_[:], scalar=self.smc("kkT", oi * HC + hc), in1=t2[:], op0=ALU.mult, op1=ALU.mult),
                 reads=[k_, t2, self.osm], writes=[kk_])
            S.op("dve", lambda h: h.tensor_scalar(out=t1[:], in0=a_[:], scalar1=-1.0, scalar2=self.smc("kaT", oi * HC + hc), op0=ALU.add, op1=ALU.mult),
                 reads=[a_, self.osm], writes=[t1])
            S.op("dve", lambda h: h.scalar_tensor_tensor(out=k_[:], in0=t1[:], scalar=1.0, in1=k_[:], op0=ALU.add, op1=ALU.mult), reads=[t1, k_], writes=[k_])
            S.op("dve", lambda h: h.tensor_tensor(out=b_[:], in0=kk_[:], in1=a_[:], op=ALU.mult), reads=[kk_, a_], writes=[b_])
            S.op("dve", lambda h: h.scalar_tensor_tensor(out=t1[:], in0=r_[:], scalar=self.smc("rkT", oi * HC + hc), in1=k_[:], op0=ALU.mult, op1=ALU.mult),
                 reads=[r_, k_, self.osm], writes=[t1])
            for hf in range(NH):
                pb = pbA[hf]
                S.mm(pb, pb[:, 0:TH], [(self.bdones[:], t1[:, hsl(hf)])], reads=[self.bdones, t1])
                S.op("dve", lambda h: h.tensor_tensor(out=bonus[:, hsl(hf)], in0=pb[:, 0:TH], in1=v_[:, hsl(hf)], op=ALU.mult), reads=[pb, v_], writes=[bonus], merge=(hf > 0))
            for d in range(2):
                pre, ex, inc = TP.next(), TP.next(), TP.next()
                for ch in range(NCH):
                    cs = slice(ch * 128, (ch + 1) * 128)
                    S.op("dve", lambda h: h.tensor_tensor_scan(out=pre[:, cs], data0=self.onesF[:], data1=lw[d][:, cs], initial=0.0, op0=ALU.mult, op1=ALU.add),
                         reads=[self.onesF, lw[d]], writes=[pre], merge=(ch > 0))
                if d == 0:
                    S.op("dve", lambda h: h.tensor_tensor(out=ex[:], in0=pre[:], in1=lw[d][:], op=ALU.subtract), reads=[pre, lw[d]], writes=[ex])
                    inc = pre
                else:
                    for ch in range(NCH):
                        cs = slice(ch * 128, (ch + 1) * 128)
                        S.op("dve", lambda h: h.tensor_scalar(out=ex[:, cs], in0=pre[:, cs], scalar1=pre[:, ch * 128 + 127:ch * 128 + 128], scalar2=-1.0, op0=ALU.subtract, op1=ALU.mult),
                             reads=[pre], writes=[ex], merge=(ch > 0))
                    S.op("dve", lambda h: h.tensor_tensor(out=inc[:], in0=ex[:], in1=lw[d][:], op=ALU.add), reads=[ex, lw[d]], writes=[inc])
                e = TP.next()
                S.op("act", lambda h: h.activation(out=e[:], in_=ex[:], func=AF.Exp), reads=[ex], writes=[e])
                S.op("dve", lambda h: h.scalar_tensor_tensor(out=At[d][:], in0=kk_[:], scalar=-1.0, in1=e[:], op0=ALU.mult, op1=ALU.mult), reads=[kk_, e], writes=[At[d]])
                e2 = TP.next()
                S.op("act", lambda h: h.activation(out=e2[:], in_=inc[:], func=AF.Exp), reads=[inc], writes=[e2])
                S.op("dve", lambda h: h.tensor_tensor(out=Rt[d][:], in0=r_[:], in1=e2[:], op=ALU.mult), reads=[r_, e2], writes=[Rt[d]])
                col = 127 if d == 0 else 0
                S.op("dve", lambda h: h.tensor_copy(out=pC[d][:], in_=e2[:, col:NT:128]), reads=[e2], writes=[pC[d]])
                S.op("act", lambda h: h.activation(out=e[:], in_=inc[:], func=AF.Exp, scale=-1.0), reads=[inc], writes=[e])
                S.op("dve", lambda h: h.tensor_tensor(out=Bt[d][:], in0=b_[:], in1=e[:], op=ALU.mult), reads=[b_, e], writes=[Bt[d]])
                S.op("dve", lambda h: h.tensor_tensor(out=Kt[d][:], in0=k_[:], in1=e[:], op=ALU.mult), reads=[k_, e], writes=[Kt[d]])
            S.barrier()
            for step in range(NCH):
                gens = [self._rwkv_item(oi, hc, d, hp, step, At, Rt, Bt, Kt, pC, v_, ST, Send, tm, IT[(d, hp)], GT0s, H0p, RhT, yacc, mask_strict, mask_incl)
                        for d in range(2) for hp in range(2)]
                while gens:
                    for gq in list(gens):
                        try:
                            next(gq)
                        except StopIteration:
                            gens.remove(gq)
            S.barrier()
            for hf in range(NH):
                sl = hsl(hf)
                p1, p2 = pbA[hf], pbB[hf]
                ysq, mean, tt_ = TP.next(), TP.next(), TP.next()
                S.op("act", lambda h: h.activation(out=ysq[:, sl], in_=yacc[:, sl], func=AF.Square), reads=[yacc], writes=[ysq])
                S.mm(p1, p1[:, 0:TH], [(self.bdones[:], yacc[:, sl])], reads=[self.bdones, yacc])
                S.mm(p2, p2[:, 0:TH], [(self.bdones[:], ysq[:, sl])], reads=[self.bdones, ysq])
                S.op("act", lambda h: h.activation(out=mean[:, sl], in_=p1[:, 0:TH], func=AF.Copy, scale=1.0 / 64), reads=[p1], writes=[mean])
                S.op("dve", lambda h: h.tensor_tensor(out=ysq[:, sl], in0=mean[:, sl], in1=mean[:, sl], op=ALU.mult), reads=[mean], writes=[ysq])
                S.op("dve", lambda h: h.scalar_tensor_tensor(out=tt_[:, sl], in0=p2[:, 0:TH], scalar=1.0 / 64, in1=ysq[:, sl], op0=ALU.mult, op1=ALU.subtract), reads=[p2, ysq], writes=[tt_])
                S.op("dve", lambda h: h.tensor_scalar(out=tt_[:, sl], in0=tt_[:, sl], scalar1=64e-5, scalar2=None, op0=ALU.add), reads=[tt_], writes=[tt_])
                S.op("act", lambda h: h.activation(out=tt_[:, sl], in_=tt_[:, sl], func=AF.Sqrt), reads=[tt_], writes=[tt_])
                S.op("dve", lambda h: h.reciprocal(out=tt_[:, sl], in_=tt_[:, sl]), reads=[tt_], writes=[tt_])
                S.op("dve", lambda h: h.tensor_tensor(out=ysq[:, sl], in0=yacc[:, sl], in1=mean[:, sl], op=ALU.subtract), reads=[yacc, mean], writes=[ysq])
                S.op("dve", lambda h: h.tensor_tensor(out=ysq[:, sl], in0=ysq[:, sl], in1=tt_[:, sl], op=ALU.mult), reads=[ysq, tt_], writes=[ysq])
                S.op("act", lambda h: h.activation(out=ysq[:, sl], in_=ysq[:, sl], func=AF.Identity, scale=self.smc("gngT", oi * HC + hc), bias=self.smc("gnbT", oi * HC + hc)),
                     reads=[ysq, self.osm], writes=[ysq])
                S.op("dve", lambda h: h.tensor_tensor(out=ysq[:, sl], in0=ysq[:, sl], in1=bonus[:, sl], op=ALU.add), reads=[ysq, bonus], writes=[ysq])
                ob = self.oslot.next()
                S.op("dve", lambda h: h.tensor_tensor(out=ob[:], in0=ysq[:, sl], in1=g_[:, sl], op=ALU.mult), reads=[ysq, g_], writes=[ob])
                S.dma("sp", self.MIX[hc * 128:(hc + 1) * 128, sl], ob[:], ob, reads=[ob], writes=[MIXr], merge=True)
            S.barrier()

    def _rwkv_item(self, oi, hc, d, hp, step, At, Rt, Bt, Kt, pC, v_, ST, Send, tm, it, GT0s, H0p, RhT, yacc, mask_strict, mask_incl):
        c, S = self.cfg, self.S
        NCH, HC = c.NCH, c.HC
        ch = step if d == 0 else NCH - 1 - step
        cs = slice(ch * 128, (ch + 1) * 128)
        ps = slice(hp * 64, hp * 64 + 64)
        bk = self.bank[d * 2 + hp]
        q = [bk[:, i * 128:(i + 1) * 128] for i in range(4)]
        At_tm, Bt_tm, Kt_tm, V_tm = tm[d]
        ms, mi_ = self.masks[:, mask_strict[d], :], self.masks[:, mask_incl[d], :]
        ms_t = self.masks[:, mask_strict[1 - d], :]
        if hp == 0:
            tb = self.bank[4 + d]
            for i, src in enumerate((At[d], Bt[d], Kt[d], v_)):
                S.mm(tb, tb[:, i * 128:(i + 1) * 128], [(src[:, cs], self.identF[:])], reads=[src, self.identF], transpose=True, merge=(i > 0))
            for i, dst in enumerate(tm[d]):
                if i % 2 == 0:
                    S.op("act", lambda h: h.activation(out=dst[:], in_=tb[:, i * 128:(i + 1) * 128], func=AF.Copy), reads=[tb], writes=[dst])
                else:
                    S.op("dve", lambda h: h.tensor_copy(out=dst[:], in_=tb[:, i * 128:(i + 1) * 128]), reads=[tb], writes=[dst])
        yield
        PT, P, X = it["NT_"], it["N"], it["X"]
        a_ps, b_ps, k_ps, r_ps = At[d][ps, cs], Bt[d][ps, cs], Kt[d][ps, cs], Rt[d][ps, cs]
        S.mm(bk, q[0], [(b_ps, a_ps)], reads=[Bt[d], At[d]])
        S.mm(bk, q[1], [(a_ps, b_ps)], reads=[Bt[d], At[d]], merge=True)
        S.mm(bk, q[2], [(k_ps, a_ps)], reads=[Kt[d], At[d]], merge=True)
        S.mm(bk, q[3], [(b_ps, r_ps)], reads=[Bt[d], Rt[d]], merge=True)
        S.op("dve", lambda h: h.tensor_tensor(out=PT[0][:], in0=q[0], in1=ms, op=ALU.mult), reads=[bk, self.masks], writes=[PT[0]])
        S.op("dve", lambda h: h.tensor_tensor(out=P[0][:], in0=q[1], in1=ms_t, op=ALU.mult), reads=[bk, self.masks], writes=[P[0]])
        S.op("dve", lambda h: h.tensor_tensor(out=it["MkT"][:], in0=q[2], in1=ms, op=ALU.mult), reads=[bk, self.masks], writes=[it["MkT"]])
        S.op("dve", lambda h: h.tensor_tensor(out=it["QbT"][:], in0=q[3], in1=mi_, op=ALU.mult), reads=[bk, self.masks], writes=[it["QbT"]])
        yield
        S.mm(bk, q[0], [(k_ps, r_ps)], reads=[Kt[d], Rt[d]])
        S.mm(bk, q[1][:, 0:64], [(it["MkT"][:], V_tm[:, ps])], reads=[it["MkT"], V_tm], merge=True)
        S.op("dve", lambda h: h.tensor_tensor(out=it["QkT"][:], in0=q[0], in1=mi_, op=ALU.mult), reads=[bk, self.masks], writes=[it["QkT"]])
        S.op("act", lambda h: h.activation(out=X[0][:, 0:64], in_=At_tm[:, ps], func=AF.Copy), reads=[At_tm], writes=[X[0]])
        S.op("dve", lambda h: h.tensor_copy(out=X[0][:, 64:128], in_=q[1][:, 0:64]), reads=[bk], writes=[X[0]], merge=True)
        yield
        cur = 0
        for lev in range(7):
            nxt = 1 - cur
            S.mm(bk, q[0], [(PT[cur][:], X[cur][:])], reads=[PT[cur], X[cur]])
            if lev < 6:
                S.mm(bk, q[1], [(PT[cur][:], P[cur][:])], reads=[PT[cur], P[cur]], merge=True)
                S.mm(bk, q[2], [(P[cur][:], PT[cur][:])], reads=[PT[cur], P[cur]], merge=True)
            S.op("dve", lambda h: h.tensor_tensor(out=X[nxt][:], in0=q[0], in1=X[cur][:], op=ALU.add), reads=[bk, X[cur]], writes=[X[nxt]])
            if lev < 6:
                S.op("act", lambda h: h.activation(out=P[nxt][:], in_=q[1], func=AF.Copy), reads=[bk], writes=[P[nxt]])
                S.op("act", lambda h: h.activation(out=PT[nxt][:], in_=q[2], func=AF.Copy), reads=[bk], writes=[PT[nxt]])
            cur = nxt
            yield
        Xf = X[cur]
        ci = ch
        S.mm(bk, bk[ps, 0:64], [(Xf[:, 0:64], Bt_tm[:, ps])], reads=[Xf, Bt_tm])
        S.mm(bk, bk[ps, 64:128], [(Bt_tm[:, ps], Xf[:, 64:128]), (Kt_tm[:, ps], V_tm[:, ps])], reads=[Xf, Bt_tm, Kt_tm, V_tm], merge=True)
        S.mm(bk, bk[ps, 128:256], [(Xf[:, 0:64], it["QbT"][:])], reads=[Xf, it["QbT"]], merge=True)
        S.op("dve", lambda h: h.tensor_tensor(out=GT0s[d][ps, :], in0=bk[ps, 0:64], in1=self.identF[ps, ps], op=ALU.add), reads=[bk, self.identF], writes=[GT0s[d]], merge=True)
        S.op("dve", lambda h: h.tensor_scalar(out=H0p[d][ps, :], in0=bk[ps, 64:128], scalar1=pC[d][ps, ci:ci + 1], scalar2=None, op0=ALU.mult), reads=[bk, pC[d]], writes=[H0p[d]], merge=True)
        S.op("dve", lambda h: h.tensor_tensor(out=RhT[d][ps, :], in0=bk[ps, 128:256], in1=Rt[d][ps, cs], op=ALU.add), reads=[bk, Rt[d]], writes=[RhT[d]], merge=True)
        yield
        yb = self.bank[6 + d]
        S.mm(yb, yb[ps, 128:256], [(Xf[:, 64:128], it["QbT"][:]), (V_tm[:, ps], it["QkT"][:]), (ST[d][ps, :], RhT[d][ps, :])],
             reads=[Xf, it["QbT"], V_tm, it["QkT"], ST[d], RhT[d]], merge=(hp > 0))
        S.mm(yb, yb[ps, 0:64], [(GT0s[d][ps, :], ST[d][ps, :])], reads=[GT0s[d], ST[d]], merge=True)
        yield
        if hp == 1:
            S.op("dve", lambda h: h.tensor_tensor(out=yacc[:, cs], in0=yb[:, 128:256], in1=yacc[:, cs], op=ALU.add), reads=[yb, yacc], writes=[yacc])
            se = Send[d].next()
            S.op("dve", lambda h: h.scalar_tensor_tensor(out=se[:], in0=yb[:, 0:64], scalar=pC[d][:, ci:ci + 1], in1=H0p[d][:], op0=ALU.mult, op1=ALU.add),
                 reads=[yb, pC[d], H0p[d]], writes=[se])
            if step + 1 < NCH:
                S.op("dve", lambda h: h.tensor_scalar(out=ST[d][:], in0=se[:], scalar1=self.smc("mres", d * NCH + step), scalar2=None, op0=ALU.mult),
                     reads=[se, self.osm], writes=[ST[d]])
            r0 = ((oi * 2 + d) * NCH + ch) * 128
            S.dma("sp", self.dout["st_out"][r0:r0 + 128, hc * 64:(hc + 1) * 64], se[:], se, reads=[se])

    def dump(self, name, buf, shape):
        o = self.outp(name, shape)
        self.S.dma("sp", o[:, :], buf[:], buf, reads=[buf])

    def build(self):
        c, S = self.cfg, self.S
        self.phase_init()
        self.phase_ada()
        if getattr(c, "debug", False):
            self.dump("dbg_mod", self.modT, [128, c.L * 9 * c.DC])
            self.dump("dbg_sc1", self.sc1T, [128, c.L * 9 * c.DC])
        k = 0
        done = False
        for l in range(c.L):
            for s in range(3):
                if c.stop_after is not None and k >= c.stop_after:
                    done = True
                    break
                Xin, Xout = self.X[k], self.X[k + 1]
                if s != 1:
                    self.phase_ffn(l, s, Xin, Xout)
                elif l % 2 == 0:
                    self.phase_even(l, Xin, Xout)
                else:
                    self.phase_odd(l, Xin, Xout)
                k += 1
            if done:
                break
        if k < c.L * 3:
            src = self.X[k]
            S.barrier()
            for ch in range(c.DC):
                o = self.oslot.next()
                S.dma("sp", o[:, 0:c.TH], src[ch * 128:(ch + 1) * 128, 0:c.TH], o, writes=[o])
                S.dma("sp", self.dout["yT"][ch * 128:(ch + 1) * 128, 0:c.TH], o[:, 0:c.TH], o, reads=[o])
                for hf in range(1, c.NH):
                    o = self.oslot.next()
                    S.dma("sp", o[:, 0:c.TH], src[ch * 128:(ch + 1) * 128, hf * c.TH:(hf + 1) * c.TH], o, writes=[o])
                    S.dma("sp", self.dout["yT"][ch * 128:(ch + 1) * 128, hf * c.TH:(hf + 1) * c.TH], o[:, 0:c.TH], o, reads=[o])
        S.finish()
        return self.nc


def tlay(v, nch=None):
    v = np.asarray(v, np.float32).reshape(-1, 128)
    return np.ascontiguousarray(v.T)


def prep_shared(cfg, inp):
    c = cfg
    D, DC, FC, L = c.D, c.DC, c.FC, c.L
    sh = {}
    sh["w_ada"] = np.ascontiguousarray(inp["w_ada"].reshape(L * D, 9 * D))
    sh["b_adaT"] = np.concatenate([tlay(inp["b_ada"][l]) for l in range(L)], axis=1)
    sh["lnT"] = np.concatenate([tlay(inp[k][l, s]) for l in range(L) for s in range(3) for k in ("ln_g", "ln_b")], axis=1)
    w = inp["ffn_w_in"].reshape(L * 2, DC, 128, 2, FC, 128)
    sh["ffn_in"] = np.ascontiguousarray(w.transpose(0, 4, 2, 1, 3, 5)).reshape(L * 2 * FC * 128, DC * 256)
    w = inp["ffn_w_out"].reshape(L * 2, FC, 128, DC, 128)
    sh["ffn_out"] = np.ascontiguousarray(w.transpose(0, 3, 2, 1, 4)).reshape(L * 2 * DC * 128, FC * 128)
    sh["ident"] = np.eye(128, dtype=np.float32)
    NE = (L + 1) // 2
    MC = c.MIXW // 128
    sh["ev_in"] = np.ascontiguousarray(inp["even_w_in"].reshape(NE * D, c.MIXW))
    w = inp["even_w_out"].reshape(NE, MC, 128, DC, 128)
    sh["ev_out"] = np.ascontiguousarray(w.transpose(0, 3, 2, 1, 4)).reshape(NE * DC * 128, MC * 128)
    sh["pool_w"] = np.ascontiguousarray(inp["pool_w"].reshape(NE * 4 * c.GW, c.GW))
    sh["pool_scT"] = np.concatenate([tlay(inp["pool_scale"][i]) for i in range(NE)], axis=1)
    kc = np.arange(c.GW, dtype=np.float64)
    ang = 2 * np.pi * np.outer(kc, kc) / c.GW
    sh["cosC"] = np.ascontiguousarray(np.cos(ang).reshape(c.GC, 128, c.GW).transpose(1, 0, 2)).reshape(128, -1).astype(np.float32)
    sh["nsinC"] = np.ascontiguousarray((-np.sin(ang)).reshape(c.GC, 128, c.GW).transpose(1, 0, 2)).reshape(128, -1).astype(np.float32)
    prep_shared_odd(cfg, inp, sh)
    return sh


def prep_shared_odd(cfg, inp, sh):
    c = cfg
    NO = c.L // 2
    if NO == 0:
        return
    D, DC, HC, RW, GL = c.D, c.DC, c.HC, c.RW, c.GATE_LORA
    MC = c.MIXW // 128
    od_in, od_inv, muT = [], [], []
    for oi in range(NO):
        W = inp["odd_w_in"][oi]
        mu = inp["shift_mu"][oi]
        ranges = [(i * 128, 128) for i in range(3 * HC)]
        g0 = 3 * RW
        for gc in range(c.GLC):
            ranges.append((g0 + gc * 128, min(128, GL - gc * 128)))
        w0 = g0 + GL
        ranges += [(w0, 128), (w0 + 128, 128), (w0 + 256, 128)]
        nrw = len(ranges)
        q0 = c.RCOLS
        ranges += [(q0 + i * 128, 128) for i in range(c.NQ)]
        k0 = q0 + c.ATT_W
        ranges += [(k0 + i * 128, 128) for i in range(c.NKV)]
        blocks = np.zeros((len(ranges), 128, DC, 128), np.float32)
        mus = np.zeros((128, nrw), np.float32)
        for ci, (c0, wd) in enumerate(ranges):
            blk = W[:, c0:c0 + wd].reshape(DC, 128, wd)
            blocks[ci, :, :, :wd] = blk.transpose(1, 0, 2)
            if ci < nrw:
                mus[:wd, ci] = mu[c0:c0 + wd]
        od_in.append(blocks.reshape(len(ranges) * 128, DC * 128))
        od_inv.append(np.ascontiguousarray(W[:, k0 + c.KVW:k0 + 2 * c.KVW]))
        muT.append(mus)
    sh["od_in"] = np.concatenate(od_in, 0)
    sh["od_inv"] = np.concatenate(od_inv, 0)
    sh["muT"] = np.concatenate(muT, 1)
    w = inp["odd_w_out"].reshape(NO, MC, 128, DC, 128)
    sh["od_out"] = np.ascontiguousarray(w.transpose(0, 3, 2, 1, 4)).reshape(NO * DC * 128, MC * 128)
    sh["w0T"] = np.concatenate([tlay(inp["decay_w0"][oi, d]) for oi in range(NO) for d in range(2)], 1)
    sh["dw2"] = np.ascontiguousarray(inp["decay_w2"].reshape(NO * 2 * 128, RW))
    sh["a0T"] = np.concatenate([tlay(inp["iclr_a0"][oi]) for oi in range(NO)], 1)
    sh["iw2"] = np.ascontiguousarray(inp["iclr_w2"].reshape(NO * 128, RW))
    gw = np.zeros((NO, c.GLC * 128, RW), np.float32)
    gw[:, :GL] = inp["gate_w2"]
    sh["gw2"] = gw.reshape(NO * c.GLC * 128, RW)
    for nm, key in (("kkT", "k_k"), ("kaT", "k_a"), ("gngT", "gn_g"), ("gnbT", "gn_b")):
        sh[nm] = np.concatenate([tlay(inp[key][oi]) for oi in range(NO)], 1)
    sh["rkT"] = np.concatenate([tlay(inp["r_k"][oi].reshape(-1)) for oi in range(NO)], 1)
    sh["qnT"] = np.concatenate([tlay(inp["q_norm"][oi]) for oi in range(NO)], 1)
    sh["knT"] = np.concatenate([tlay(inp["k_norm"][oi]) for oi in range(NO)], 1)
    r = np.arange(128)
    M = np.stack([(r[:, None] < r[None, :]), (r[:, None] > r[None, :]), (r[:, None] <= r[None, :]), (r[:, None] >= r[None, :])], 1).astype(np.float32)
    sh["masks"] = np.ascontiguousarray(M).reshape(128, 4 * 128)
    sh["bdones"] = ((r[:, None] // 64) == (r[None, :] // 64)).astype(np.float32)
    HD = c.HD
    half, quarter = HD // 2, HD // 4
    P = np.zeros((HD, HD), np.float32)
    for dd in range(HD):
        if dd % half < quarter:
            P[dd, dd + quarter] = -1.0
        else:
            P[dd, dd - quarter] = 1.0
    sh["rotT"] = np.ascontiguousarray(P.T)


def core_consts_odd(cfg, inp, ci, is_s, m):
    c = cfg
    NO = c.L // 2
    if NO == 0:
        return
    NT, HD, NCH, HC = c.NT, c.HD, c.NCH, c.HC
    nseq, slen = seq_struct(c, is_s)
    t = np.arange(NT)
    sid, pos = t // slen, t % slen
    m["mprev"] = np.ascontiguousarray(np.broadcast_to((pos > 0).astype(np.float32), (128, NT)))
    m["mnext"] = np.ascontiguousarray(np.broadcast_to((pos < slen - 1).astype(np.float32), (128, NT)))
    half, quarter = HD // 2, HD // 4
    C = np.ones((HD, NT), np.float32)
    Sn = np.zeros((HD, NT), np.float32)
    if is_s:
        inv = (10000.0 ** (-np.arange(quarter, dtype=np.float32) / quarter)).astype(np.float32)
        row = (t // c.GRID_W).astype(np.float32)
        col = (t % c.GRID_W).astype(np.float32)
        for dd in range(HD):
            p_ = row if dd // half == 0 else col
            ang = (p_ * inv[dd % quarter]).astype(np.float32)
            C[dd] = np.cos(ang)
            Sn[dd] = np.sin(ang)
    m["ropeC"], m["ropeS"] = C, Sn
    BIG = 8192.0
    Am = np.zeros((8, NT), np.float32)
    Bm = np.zeros((8, c.NKEY), np.float32)
    for r in range(nseq):
        Am[r, sid == r] = BIG
        Bm[r, c.PAST:][sid == r] = 1.0
    if is_s:
        Bm[0, :c.PAST] = 1.0
    Am[7] = -BIG
    Bm[7] = 1.0
    m["Am"], m["Bm"] = Am, Bm
    cKT = np.zeros((NO * c.NKV * 128, c.PAST), np.float32)
    cV = np.zeros((NO * c.PAST, c.KVW), np.float32)
    st0 = np.zeros((NO * 2 * 128, HC * 64), np.float32)
    if is_s:
        for oi in range(NO):
            cKT[oi * c.NKV * 128:(oi + 1) * c.NKV * 128] = inp["cache_k"][ci, oi].transpose(1, 2, 0).reshape(c.NKV * 128, c.PAST)
            cV[oi * c.PAST:(oi + 1) * c.PAST] = inp["cache_v"][ci, oi].reshape(c.PAST, c.KVW)
            for d in range(2):
                a = inp["state_rwkv"][ci, oi, d].reshape(HC, 2, 64, 64)
                st0[(oi * 2 + d) * 128:(oi * 2 + d + 1) * 128] = a.transpose(1, 3, 0, 2).reshape(128, HC * 64)
    m["cKT"], m["cV"], m["st0"] = cKT, cV, st0
    mres = np.ones((128, 2 * NCH), np.float32)
    for k in range(NCH):
        if ((k + 1) * 128) % slen == 0:
            mres[:, k] = 0.0
        ch = NCH - 1 - k
        if (ch * 128) % slen == 0:
            mres[:, NCH + k] = 0.0
    m["mres"] = mres


def seq_struct(cfg, is_s):
    c = cfg
    slen = c.NT if is_s else c.SEQ
    return c.NT // slen, slen


def core_consts(cfg, is_s):
    c = cfg
    NT, TT = c.NT, c.TT
    nseq, slen = seq_struct(c, is_s)
    m = {}
    t = np.arange(NT)
    sid = t // slen
    pos = t % slen
    same = sid[:, None] == sid[None, :]
    pm = np.zeros((4, 128, TT, 3, 128), np.float32)
    for wi, w in enumerate(c.POOL_WINDOWS):
        lo = np.clip(pos - w // 2, 0, slen) + sid * slen
        hi = np.clip(pos + w // 2, 0, slen) + sid * slen
        M = ((t[None, :] >= lo[:, None]) & (t[None, :] < hi[:, None])).astype(np.float64) / (hi - lo)[:, None]
        M = M - np.eye(NT)
        for j in range(TT):
            for nb in range(3):
                i = j + nb - 1
                if 0 <= i < TT:
                    pm[wi, :, j, nb, :] = M[j * 128:(j + 1) * 128, i * 128:(i + 1) * 128].T
    m["poolM"] = np.ascontiguousarray(pm.transpose(1, 0, 2, 3, 4)).reshape(128, -1)
    ang = 2 * np.pi * np.outer(pos, pos).astype(np.float64) / slen
    sc = (slen * c.GW) ** -0.5
    CN = np.where(same, np.cos(ang), 0.0) * sc
    SN = np.where(same, np.sin(ang), 0.0) * sc
    m["cosN"] = np.ascontiguousarray(CN.reshape(TT, 128, NT).transpose(1, 0, 2)).reshape(128, -1).astype(np.float32)
    m["sinN"] = np.ascontiguousarray(SN.reshape(TT, 128, NT).transpose(1, 0, 2)).reshape(128, -1).astype(np.float32)
    return m


def core_tokens(cfg, inp, ci):
    c = cfg
    if ci < c.NSAMP:
        return inp["x_sample"][ci], inp["c"][ci], True
    pc = ci - c.NSAMP
    nps = c.NT // c.SEQ
    return inp["x_prompt"][pc * nps:(pc + 1) * nps].reshape(c.NT, c.D), inp["c_ctx"], False


def prep_core(cfg, inp, ci, sh):
    x, cond, is_s = core_tokens(cfg, inp, ci)
    m = dict(sh)
    m["xT"] = np.ascontiguousarray(x.T)
    m["cond"] = tlay(cond)
    m.update(core_consts(cfg, is_s))
    core_consts_odd(cfg, inp, ci, is_s, m)
    return m


def run_cfg(cfg, inputs):
    c = cfg
    inp = {k: np.asarray(v) for k, v in inputs.items()}
    prog = Program(c)
    nc = prog.build()
    sh = prep_shared(c, inp)
    ncores = c.NSAMP + c.NPROMPT_CORES
    maps = []
    for ci in range(ncores):
        m = prep_core(c, inp, ci, sh)
        mm = {}
        for k, ap in prog.din.items():
            a = np.asarray(m[k], dtype=np.float32)
            shp = tuple(int(x) for x in ap.shape)
            if a.shape != shp:
                a = a.reshape(shp)
            mm[k] = np.ascontiguousarray(a)
        maps.append(mm)
    res = run_bass_kernel_spmd(nc, maps, core_ids=list(range(ncores))).results
    NO, NT, D, SEQ = c.L // 2, c.NT, c.D, c.SEQ
    nps = NT // SEQ
    nprompt = c.NPROMPT_CORES * nps
    y_sample = np.stack([np.ascontiguousarray(res[ci]["yT"].T) for ci in range(c.NSAMP)], 0).astype(np.float32)
    y_prompt = np.zeros((nprompt, SEQ, D), np.float32)
    nk = np.zeros((nprompt, NO, SEQ, c.NKV, c.HD), np.float32)
    nv = np.zeros((nprompt, NO, SEQ, c.NKV, c.HD), np.float32)
    ns = np.zeros((nprompt, NO, 2, c.RH, 64, 64), np.float32)
    cps = SEQ // 128
    for pc in range(c.NPROMPT_CORES):
        r = res[c.NSAMP + pc]
        y_prompt[pc * nps:(pc + 1) * nps] = r["yT"].T.reshape(nps, SEQ, D)
        kT = r["kT_out"].reshape(NO, c.NKV, c.HD, NT)
        vo = r["v_out"].reshape(NO, NT, c.NKV, c.HD)
        so = r["st_out"].reshape(NO, 2, c.NCH, 2, 64, c.HC, 64)
        for s_ in range(nps):
            b = pc * nps + s_
            for oi in range(NO):
                nk[b, oi] = kT[oi][:, :, s_ * SEQ:(s_ + 1) * SEQ].transpose(2, 0, 1)
                nv[b, oi] = vo[oi, s_ * SEQ:(s_ + 1) * SEQ]
                for d in range(2):
                    ch = (s_ + 1) * cps - 1 if d == 0 else s_ * cps
                    ns[b, oi, d] = so[oi, d, ch].transpose(2, 0, 3, 1).reshape(c.RH, 64, 64)
    return (y_prompt, y_sample, nk, nv, ns)


def kernel(**inputs):
    return run_cfg(FULL, inputs)
```
